# Optimizing a Trainium2 kernel written in Bass

```python
import jax, jax.numpy as jnp
from jax import lax
import numpy as np

D_MODEL = 2048
BATCH = 4
SEQ = 2048
DEPTH = 1
DEC_BATCH = 128
DEC_SEQ = 1
PAST_LEN = 16384
PAGE_SIZE = 128

N_META = 16
MIX_WIDTH = D_MODEL
A_WIDTH = MIX_WIDTH // 2
B_WIDTH = MIX_WIDTH - A_WIDTH
A_EXPAND = 128
A_HEADS = A_WIDTH // A_EXPAND
A_DK = A_EXPAND
A_DV = A_WIDTH // A_HEADS
CHUNK = 64
CONV_W = 31
B_GROUPS = 8
D_FF = -(-8 * D_MODEL // (3 * 256)) * 256
IN_COLS = 4 * A_WIDTH + 2 * B_WIDTH
EPS = 1e-6

kernel_name = "hymba_hgrn2_conformer_conv_decode_step"


def _rmsnorm(x, g):
    xf = x.astype(jnp.float32)
    y = xf * lax.rsqrt(jnp.mean(xf * xf, axis=-1, keepdims=True) + EPS)
    return (y * g.astype(jnp.float32)).astype(x.dtype)


def _gla_chunk(S0, q, k, v, logf):
    C = q.shape[2]
    b = jnp.cumsum(logf, axis=2)
    mask = jnp.tril(jnp.ones((C, C), dtype=bool))[None, None, :, :, None]
    diff = b[:, :, :, None, :] - b[:, :, None, :, :]
    decay = jnp.exp(jnp.where(mask, diff, -jnp.inf))
    scores = jnp.einsum('bhtk,bhsk,bhtsk->bhts', q, k, decay)
    o = jnp.einsum('bhts,bhsv->bhtv', scores, v) + \
        jnp.einsum('bhtk,bhkv->bhtv', q * jnp.exp(b), S0)
    b_last = b[:, :, -1, :]
    S_new = jnp.exp(b_last)[..., None] * S0 + \
        jnp.einsum('bhsk,bhsv->bhkv', k * jnp.exp(b_last[:, :, None, :] - b), v)
    return o, S_new


def _hgrn_prompt(q, k, v, logf):
    B, H, T, _ = q.shape
    S0 = jnp.zeros((B, H, A_DK, A_DV), jnp.float32)
    o_meta, S = _gla_chunk(S0, q[:, :, :N_META], k[:, :, :N_META], v[:, :, :N_META], logf[:, :, :N_META])
    n_chunks = (T - N_META) // CHUNK

    def to_chunks(t):
        t = t[:, :, N_META:]
        return t.reshape(B, H, n_chunks, CHUNK, t.shape[-1]).transpose(2, 0, 1, 3, 4)

    def step(S_c, xs):
        qc, kc, vc, fc = xs
        o_c, S_n = _gla_chunk(S_c, qc, kc, vc, fc)
        return S_n, o_c

    S, o_rest = lax.scan(step, S, (to_chunks(q), to_chunks(k), to_chunks(v), to_chunks(logf)))
    o_rest = o_rest.transpose(1, 2, 0, 3, 4).reshape(B, H, n_chunks * CHUNK, A_DV)
    return jnp.concatenate([o_meta, o_rest], axis=2), S


def _layer(h, conv_buf, S0, is_prompt, lb, g_mix, w_in, hgrn_g, conv_w, conv_b, gn_g, gn_b,
           w_out, g_ffn, w_gate, w_up, w_down):
    B, T, _ = h.shape
    f32 = jnp.float32
    z = _rmsnorm(h, g_mix) @ w_in
    q, f, i, og, ga, gb = jnp.split(
        z, [A_WIDTH, 2 * A_WIDTH, 3 * A_WIDTH, 4 * A_WIDTH, 4 * A_WIDTH + B_WIDTH], axis=-1)

    def heads(t, d):
        return t.reshape(B, T, A_HEADS, d).transpose(0, 2, 1, 3).astype(f32)

    fgate = lb + (1.0 - lb) * jax.nn.sigmoid(f.astype(f32))
    qh = jax.nn.silu(heads(q, A_DK))
    kh = heads(1.0 - fgate, A_DK)
    logf = heads(jnp.log(fgate), A_DK)
    vh = heads(i, A_DV)
    if is_prompt:
        o, S_new = _hgrn_prompt(qh, kh, vh, logf)
    else:
        o, S_new = _gla_chunk(S0.astype(f32), qh, kh, vh, logf)
    o = o.transpose(0, 2, 1, 3)
    o = o * lax.rsqrt(jnp.mean(o * o, axis=-1, keepdims=True) + EPS) * hgrn_g.astype(f32).reshape(A_HEADS, A_DV)
    o_a = o.reshape(B, T, A_WIDTH) * jax.nn.silu(og.astype(f32))

    u = ga.astype(f32) * jax.nn.sigmoid(gb.astype(f32))
    ucat = jnp.concatenate([conv_buf.astype(f32), u], axis=1)
    c = lax.conv_general_dilated(ucat, conv_w.astype(f32)[:, None, :], window_strides=(1,),
                                 padding='VALID', dimension_numbers=('NWC', 'WIO', 'NWC'),
                                 feature_group_count=B_WIDTH) + conv_b.astype(f32)
    new_buf = ucat[:, -(CONV_W - 1):]
    cg = c.reshape(B, T, B_GROUPS, B_WIDTH // B_GROUPS)
    mu = jnp.mean(cg, axis=-1, keepdims=True)
    var = jnp.mean(jnp.square(cg - mu), axis=-1, keepdims=True)
    cn = ((cg - mu) * lax.rsqrt(var + EPS)).reshape(B, T, B_WIDTH) * gn_g.astype(f32) + gn_b.astype(f32)
    o_b = jax.nn.silu(cn)

    h = h + jnp.concatenate([o_a, o_b], axis=-1).astype(h.dtype) @ w_out

    hf = _rmsnorm(h, g_ffn)
    h = h + (jax.nn.silu(hf @ w_gate) * (hf @ w_up)) @ w_down
    return h, S_new, new_buf


def setup_inputs(seed: int = 0) -> dict:
    key = jax.random.key(seed)
    ks = jax.random.split(key, 20)
    n = jax.random.normal
    return {
        "x_prompt": n(ks[0], (BATCH, SEQ, D_MODEL), jnp.float32),
        "x_sample": n(ks[1], (DEC_BATCH, DEC_SEQ, D_MODEL), jnp.float32),
        "state_hgrn": 0.3 * n(ks[2], (DEPTH, DEC_BATCH, A_HEADS, A_DK, A_DV), jnp.float32),
        "state_conv": n(ks[3], (DEPTH, DEC_BATCH, CONV_W - 1, B_WIDTH), jnp.float32),
        "meta_tokens": n(ks[4], (N_META, D_MODEL), jnp.float32),
        "norm_mix_g": 1.0 + 0.02 * n(ks[5], (DEPTH, D_MODEL), jnp.float32),
        "w_in": n(ks[6], (DEPTH, D_MODEL, IN_COLS), jnp.float32) * D_MODEL ** -0.5,
        "lb_logits": 0.5 * n(ks[7], (DEPTH + 1, A_WIDTH), jnp.float32),
        "hgrn_norm_g": 1.0 + 0.02 * n(ks[8], (DEPTH, A_WIDTH), jnp.float32),
        "conv_w": n(ks[9], (DEPTH, CONV_W, B_WIDTH), jnp.float32) * CONV_W ** -0.5,
        "conv_b": 0.02 * n(ks[10], (DEPTH, B_WIDTH), jnp.float32),
        "gn_g": 1.0 + 0.02 * n(ks[11], (DEPTH, B_WIDTH), jnp.float32),
        "gn_b": 0.02 * n(ks[12], (DEPTH, B_WIDTH), jnp.float32),
        "w_out": n(ks[13], (DEPTH, MIX_WIDTH, D_MODEL), jnp.float32) * MIX_WIDTH ** -0.5,
        "norm_ffn_g": 1.0 + 0.02 * n(ks[14], (DEPTH, D_MODEL), jnp.float32),
        "w_ffn_gate": n(ks[15], (DEPTH, D_MODEL, D_FF), jnp.float32) * D_MODEL ** -0.5,
        "w_ffn_up": n(ks[16], (DEPTH, D_MODEL, D_FF), jnp.float32) * D_MODEL ** -0.5,
        "w_ffn_down": n(ks[17], (DEPTH, D_FF, D_MODEL), jnp.float32) * D_FF ** -0.5,
        "norm_final_g": 1.0 + 0.02 * n(ks[18], (D_MODEL,), jnp.float32),
    }


def reference(x_prompt, x_sample, state_hgrn, state_conv, meta_tokens, norm_mix_g, w_in, lb_logits,
              hgrn_norm_g, conv_w, conv_b, gn_g, gn_b, w_out, norm_ffn_g, w_ffn_gate, w_ffn_up,
              w_ffn_down, norm_final_g):
    lb_all = jnp.cumsum(jax.nn.softmax(lb_logits.astype(jnp.float32), axis=0), axis=0)

    hp = jnp.concatenate(
        [jnp.broadcast_to(meta_tokens.astype(x_prompt.dtype)[None], (x_prompt.shape[0], N_META, D_MODEL)), x_prompt],
        axis=1)
    hs = x_sample
    zero_buf = jnp.zeros((x_prompt.shape[0], CONV_W - 1, B_WIDTH), jnp.float32)

    sp_list, cp_list, ss_list, cs_list = [], [], [], []
    for l in range(DEPTH):
        p = (lb_all[l], norm_mix_g[l], w_in[l], hgrn_norm_g[l], conv_w[l], conv_b[l], gn_g[l], gn_b[l],
             w_out[l], norm_ffn_g[l], w_ffn_gate[l], w_ffn_up[l], w_ffn_down[l])
        hp, S_p, buf_p = _layer(hp, zero_buf, None, True, *p)
        hs, S_s, buf_s = _layer(hs, state_conv[l], state_hgrn[l], False, *p)
        sp_list.append(S_p.astype(x_prompt.dtype))
        cp_list.append(buf_p.astype(x_prompt.dtype))
        ss_list.append(S_s.astype(x_sample.dtype))
        cs_list.append(buf_s.astype(x_sample.dtype))

    y_prompt = _rmsnorm(hp, norm_final_g)[:, N_META:]
    y_sample = _rmsnorm(hs, norm_final_g)
    new_state_hgrn_prompt = jnp.stack(sp_list)
    new_state_conv_prompt = jnp.stack(cp_list)
    new_state_hgrn_sample = jnp.stack(ss_list)
    new_state_conv_sample = jnp.stack(cs_list)
    return (y_prompt, y_sample, new_state_hgrn_prompt, new_state_conv_prompt, new_state_hgrn_sample, new_state_conv_sample)
```

```python
import numpy as np
from contextlib import ExitStack
import concourse.bass as bass
import concourse.mybir as mybir
from concourse.bass_utils import run_bass_kernel_spmd

F32 = mybir.dt.float32
BF = mybir.dt.bfloat16
AF = mybir.ActivationFunctionType
ALU = mybir.AluOpType

D = 2048
DFF = 5632
NFF = DFF // 128
EPS = 1e-6
NV = 336
V_W30 = 328
NCST = 960
NW = 53200

V_LB0, V_LB1, V_HG, V_CB, V_GNG, V_GNB, V_GMIX, V_GFFN, V_CW = 0, 8, 16, 24, 32, 40, 48, 64, 80
C_ID, C_MASK, C_RM, C_ONE, C_IND = 0, 128, 256, 768, 896


class Prog:
    ENG = ["pe", "act", "dve", "pool", "sp"]

    def __init__(self, nc, es):
        self.nc = nc
        self.es = es
        self.ops = {e: [] for e in self.ENG}
        self.sem = {e: es.enter_context(nc.semaphore("s_" + e)) for e in self.ENG}
        self.waited = {e: {} for e in self.ENG}
        self.res_w = {}
        self.res_r = {}
        self.dsem = {}
        self.last_tokens = {}
        self.needed = {e: set() for e in self.ENG}

    @staticmethod
    def _tkey(tok):
        return tok[1] if tok[0] == "op" else ("dma", tok[1])

    def _filter(self, eng, deps, skip_self=False):
        best = {}
        for tok in deps:
            k = self._tkey(tok)
            if k == eng and (eng == "pe" or skip_self):
                continue
            if k not in best or best[k][2] < tok[2]:
                best[k] = tok
        waits = []
        for k, tok in best.items():
            if self.waited[eng].get(k, -1) >= tok[2]:
                continue
            self.waited[eng][k] = tok[2]
            waits.append(tok)
            if tok[0] == "op":
                self.needed[tok[1]].add(tok[2])
        return waits

    def _deps(self, eng, reads, writes, skip_self=False):
        deps = []
        for r in reads:
            if r in self.res_w:
                deps.append(self.res_w[r])
        for w in writes:
            if w in self.res_w:
                deps.append(self.res_w[w])
            deps.extend(self.res_r.get(w, []))
        return self._filter(eng, deps, skip_self)

    def _commit(self, tok, reads, writes):
        for r in reads:
            self.res_r.setdefault(r, []).append(tok)
        for w in writes:
            self.res_w[w] = tok
            self.res_r[w] = []
        self.last_tokens[self._tkey(tok)] = tok

    def add(self, eng, fn, reads=(), writes=(), skip_self=False):
        waits = self._deps(eng, reads, writes, skip_self)
        idx = len(self.ops[eng])
        tok = ("op", eng, idx)
        self.ops[eng].append([waits, fn, True])
        self._commit(tok, reads, writes)
        return tok

    def dma(self, eng, pairs, reads=(), writes=(), key=None):
        waits = self._deps(eng, reads, writes)
        if key not in self.dsem:
            self.dsem[key] = [self.es.enter_context(self.nc.semaphore("d_" + str(len(self.dsem)))), 0]
        ent = self.dsem[key]
        ent[1] += len(pairs)
        sem = ent[0]
        tok = ("dma", key, 16 * ent[1])

        def fn(e, pairs=pairs, sem=sem):
            for (o, i) in pairs:
                e.dma_start(out=o, in_=i).then_inc(sem, 16)
            return None

        self.ops[eng].append([waits, fn, False])
        self._commit(tok, reads, writes)
        return tok

    def barrier(self):
        toks = list(self.last_tokens.values())
        for e in self.ENG:
            waits = self._filter(e, [t for t in toks if self._tkey(t) != e])
            if waits:
                self.ops[e].append([waits, None, False])
        self.res_w = {}
        self.res_r = {}

    def wait_all_dma(self, eng):
        toks = [("dma", key, 16 * cnt) for key, (sem, cnt) in self.dsem.items()]
        self.ops[eng].append([toks, None, False])

    def emit(self, block):
        count = {}
        for e in self.ENG:
            c = 0
            m = {}
            for idx in range(len(self.ops[e])):
                if idx in self.needed[e]:
                    c += 1
                    m[idx] = c
            count[e] = m

        def run(eng_name):
            def body(e):
                for idx, (waits, fn, sigable) in enumerate(self.ops[eng_name]):
                    for tok in waits:
                        if tok[0] == "op":
                            e.wait_ge(self.sem[tok[1]], count[tok[1]][tok[2]])
                        else:
                            e.wait_ge(self.dsem[tok[1]][0], tok[2])
                    if fn is None:
                        continue
                    ins = fn(e)
                    if sigable and idx in count[eng_name]:
                        ins.then_inc(self.sem[eng_name], 1)
            return body

        block.tensor(run("pe"))
        block.scalar(run("act"))
        block.vector(run("dve"))
        block.gpsimd(run("pool"))
        block.sync(run("sp"))


def build_nc():
    nc = bass.Bass("TRN2", target_bir_lowering=False)
    dt = nc.dram_tensor
    xall = dt("xall", [2080, D], F32, kind="ExternalInput").ap()
    sh = dt("sh", [16, 8, 128, 128], F32, kind="ExternalInput").ap()
    sc = dt("sc", [16, 30, 1024], F32, kind="ExternalInput").ap()
    vec_d = dt("vecT", [128, NV], F32, kind="ExternalInput").ap()
    cst_d = dt("cst", [128, NCST], F32, kind="ExternalInput").ap()
    gfin_d = dt("gfin", [128, D], F32, kind="ExternalInput").ap()
    wrep_d = dt("wrep", [120, 1024], F32, kind="ExternalInput").ap()
    w_in = dt("w_in", [D, 6144], F32, kind="ExternalInput").ap()
    w_out = dt("w_out", [D, D], F32, kind="ExternalInput").ap()
    w_gate = dt("w_gate", [D, DFF], F32, kind="ExternalInput").ap()
    w_up = dt("w_up", [D, DFF], F32, kind="ExternalInput").ap()
    w_down = dt("w_down", [DFF, D], F32, kind="ExternalInput").ap()
    y_d = dt("y", [1040, D], F32, kind="ExternalOutput").ap()
    sout_d = dt("sout", [8, 128, 128], F32, kind="ExternalOutput").ap()
    ctail_d = dt("ctail", [30, 1024], F32, kind="ExternalOutput").ap()
    shn_d = dt("shn", [16, 8, 128, 128], F32, kind="ExternalOutput").ap()
    scn_d = dt("scn", [16, 30, 1024], F32, kind="ExternalOutput").ap()
    hscr = dt("hscr", [1040, D], F32).ap()
    h2scr = dt("h2scr", [1040, D], F32).ap()

    w_in_v = w_in.rearrange("(k p) n -> p k n", p=128)
    w_out_v = w_out.rearrange("(k p) n -> p k n", p=128)
    w_gate_v = w_gate.rearrange("(k p) n -> p k n", p=128)
    w_up_v = w_up.rearrange("(k p) n -> p k n", p=128)
    w_down_v = w_down.rearrange("(k p) n -> p k n", p=128)

    with ExitStack() as es:
        arena = es.enter_context(nc.sbuf_tensor("arena", [128, NW], F32))
        PA = es.enter_context(nc.psum_tensor("PA", [128, 1024], F32))
        PB = es.enter_context(nc.psum_tensor("PB", [128, 1024], F32))
        PC = es.enter_context(nc.psum_tensor("PC", [128, 1024], F32))
        PT = es.enter_context(nc.psum_tensor("PT", [128, 1024], F32))
        PTb = PT[:, :].bitcast(BF)
        P = Prog(nc, es)

        ptr = [0]

        def alloc(nwords, dtype=F32):
            o = ptr[0]
            ptr[0] += nwords
            assert ptr[0] <= NW, ("arena overflow", ptr[0])
            v = arena[:, o:o + nwords]
            if dtype == BF:
                v = v.bitcast(BF)
            return v

        def mark():
            return ptr[0]

        def reset(m):
            ptr[0] = m

        vecT = alloc(NV)
        cst = alloc(NCST)
        identb = alloc(64, BF)
        onesb = alloc(64, BF)
        indb = alloc(32, BF)
        oml = alloc(8)
        lbv = alloc(8)
        Sst = alloc(1024)
        Sbf = alloc(512, BF)
        ss1 = alloc(4)
        junk1 = alloc(8)
        Sst3 = Sst.rearrange("p (h v) -> p h v", v=128)
        Sbf3 = Sbf.rearrange("p (h v) -> p h v", v=128)
        indb3 = indb.rearrange("p (t s) -> p t s", s=16)
        ident_f = cst[:, C_ID:C_ID + 128]
        mask2 = cst[:, C_MASK:C_MASK + 128]
        rmask = cst[:, C_RM:C_RM + 512]

        P.dma("sp", [(vecT, vec_d), (cst, cst_d)], writes=["vecT", "cst"], key="cst")
        P.add("dve", lambda e: e.tensor_copy(out=identb, in_=ident_f), ["cst"], ["identb"])
        P.add("dve", lambda e: e.tensor_copy(out=onesb, in_=cst[:, C_ONE:C_ONE + 128]), ["cst"], ["onesb"])
        P.add("dve", lambda e: e.tensor_copy(out=indb, in_=cst[:, C_IND:C_IND + 64]), ["cst"], ["indb"])
        P.add("dve", lambda e: e.tensor_tensor(out=lbv, in0=vecT[:, V_LB0:V_LB0 + 8], in1=vecT[:, V_LB1:V_LB1 + 8],
                                               op=ALU.subtract), ["vecT"], ["lbv"])
        P.add("act", lambda e: e.activation(out=lbv, in_=lbv, func=AF.Sigmoid), ["lbv"], ["lbv"])
        P.add("act", lambda e: e.activation(out=oml, in_=lbv, func=AF.Identity, scale=-1.0, bias=1.0),
              ["lbv"], ["oml"])
        P.add("dve", lambda e: e.memset(Sst, 0.0), [], ["S"])
        P.add("dve", lambda e: e.memset(Sbf, 0.0), [], ["Sbf"])

        base_mark = mark()

        mixT = alloc(8320, BF).rearrange("p (c t) -> p c t", t=1040)
        R_A1 = alloc(8320)
        A1 = R_A1.bitcast(BF)
        scratch_mark = mark()
        NT = 528
        xT = A1[:, 0:16 * NT].rearrange("p (k t) -> p k t", t=NT)
        wbufs = [alloc(2048, BF).rearrange("p (k n) -> p k n", n=256) for _ in range(4)]
        R_Qt = alloc(2112)
        R_Kt = alloc(2112)
        R_KdT = alloc(2112)
        Qt = R_Qt.bitcast(BF).rearrange("p (h t) -> p h t", t=NT)
        Qb = alloc(2112, BF).rearrange("p (h t) -> p h t", t=NT)
        Kt = R_Kt.bitcast(BF).rearrange("p (h t) -> p h t", t=NT)
        KdT = R_KdT.bitcast(BF).rearrange("p (h t) -> p h t", t=NT)
        R_Kd = alloc(2560)
        R_Vt = alloc(2560)
        Kd = R_Kd.bitcast(BF).rearrange("p (i c) -> p i c", c=1024)
        Vt = R_Vt.bitcast(BF).rearrange("p (i c) -> p i c", c=1024)
        ubuf = alloc(2232, BF).rearrange("p (g t) -> p g t", t=558)
        R_D0 = alloc(2232)
        Dg = [R_D0[:, 0:1984].bitcast(BF).rearrange("p (j c) -> p j c", c=128),
              R_KdT[:, 0:1984].bitcast(BF).rearrange("p (j c) -> p j c", c=128)]
        DK = [["D0"], ["KdT%d" % h for h in range(8)]]
        JUNK_O = mark()
        tmpA = [alloc(512) for _ in range(6)]
        xin = [arena[:, JUNK_O:JUNK_O + 2048], R_A1[:, 4224:4224 + 2048]]
        XK = [["t0a", "t1a", "t2a", "t3a"], ["t0b", "t1b", "t2b", "t3b"]]
        xnb_m = arena[:, JUNK_O + 2048:JUNK_O + 3072].bitcast(BF)
        junk_m = R_A1[:, 4224 + 2048:4224 + 3072].bitcast(BF)
        tmpB = [R_A1[:, 4224 + i * 512:4224 + (i + 1) * 512] for i in range(6)]
        tset_i = [0]

        def next_tset():
            i = tset_i[0] % 2
            tset_i[0] += 1
            return ((tmpA, "a") if i == 0 else (tmpB, "b"))
        ebuf = alloc(8 * 9).rearrange("p (h c) -> p h c", c=9)
        R_scm = alloc(512)
        scm = R_scm.bitcast(BF).rearrange("p (h t) -> p h t", t=128)
        sgt = R_scm[:, 0:256]
        osq = alloc(512, BF)
        big = [alloc(1024) for _ in range(2)]
        utok = big[0]
        utok2 = big[1]
        xT_tail = alloc(240, BF).rearrange("p (k t) -> p k t", t=30)
        fgS = alloc(128).rearrange("p (h s) -> p h s", s=16)
        kgS = alloc(128).rearrange("p (h s) -> p h s", s=16)
        scin1 = R_Kd[:, 0:1024]
        wrep = R_Kd[:, 1024:2048]
        prodb = [R_Vt[:, tt * 512:(tt + 1) * 512].bitcast(BF) for tt in range(4)]
        onehot3 = R_KdT[:, 0:1024].bitcast(BF).rearrange("p (s m) -> p s m", m=128)
        S0b = [R_Qt[:, 0:1024], R_Qt[:, 1024:2048], R_KdT[:, 1024:2048], R_Kt[:, 512:1536]]
        Snbs = [R_Kt[:, 0:512].bitcast(BF), R_Kt[:, 1536:2048].bitcast(BF)]

        accs = [(PA[:, 0:512], "PA0"), (PA[:, 512:1024], "PA1"), (PB[:, 0:512], "PB0"),
                (PB[:, 512:1024], "PB1"), (PC[:, 0:512], "PC0"), (PC[:, 512:1024], "PC1")]
        acc_i = [0]

        def next_acc():
            a = accs[acc_i[0] % 6]
            acc_i[0] += 1
            return a

        wb_i = [0]

        def stream_w(view, c0, ncols=256, K=16):
            i = wb_i[0] % 4
            wb_i[0] += 1
            buf = wbufs[i]
            P.dma("pool", [(buf[:, 0:K, 0:ncols], view[:, 0:K, c0:c0 + ncols])],
                  writes=["wb%d" % i], key="wb%d" % i)
            return buf, "wb%d" % i

        PT3 = PTb.rearrange("p (k t) -> p k t", t=128)
        xin_i = [0]

        def prep_tile(src, n, dstT, t0, gcol, dst_key="xT"):
            i = xin_i[0] % 2
            xin_i[0] += 1
            xt = xin[i]
            kx = XK[i]
            P.dma("sp", [(xt[0:n, :], src)], writes=list(kx), key="ld_xin%d" % i)
            norm_transpose(xt, kx, n, dstT, t0, gcol, dst_key, xnb_m, ["t4a", "t5a"], junk_m, ["t4b", "t5b"])

        prep_scale_on_act = [False]

        def norm_transpose(xt, kx, n, dstT, t0, gcol, dst_key, xnb, xk, junk=None, jk=None):
            kx = list(kx) if isinstance(kx, (list, tuple)) else [kx]
            xk = list(xk) if isinstance(xk, (list, tuple)) else [xk]
            if junk is None:
                junk, jk = xnb, xk
            P.add("act", lambda e: e.activation(out=junk[0:n, :], in_=xt[0:n, :], func=AF.Square,
                                                accum_out=ss1[0:n, 0:1]), kx, list(jk) + ["ss1"])
            P.add("act", lambda e: e.activation(out=ss1[0:n, 1:2], in_=ss1[0:n, 0:1], func=AF.Sqrt,
                                                scale=1.0 / D, bias=EPS), ["ss1"], ["ss1b"])
            P.add("dve", lambda e: e.reciprocal(out=ss1[0:n, 2:3], in_=ss1[0:n, 1:2]), ["ss1b"], ["ss1c"])
            if prep_scale_on_act[0]:
                P.add("act", lambda e: e.activation(out=xnb[0:n, :], in_=xt[0:n, :], func=AF.Copy,
                                                    scale=ss1[0:n, 2:3]), kx + ["ss1c"], xk)
            else:
                P.add("dve", lambda e: e.tensor_scalar(out=xnb[0:n, :], in0=xt[0:n, :], scalar1=ss1[0:n, 2:3],
                                                       scalar2=None, op0=ALU.mult), kx + ["ss1c"], xk)

            def tr(e):
                ins = None
                for k in range(16):
                    ins = e.transpose(out=PT3[:, k, 0:n], in_=xnb[0:n, k * 128:(k + 1) * 128],
                                      identity=identb[0:n, 0:n])
                return ins
            P.add("pe", tr, xk + ["identb"], ["PT"])
            P.add("dve", lambda e: e.tensor_tensor(
                out=dstT[:, :, t0:t0 + n], in0=PT3[:, :, 0:n],
                in1=vecT[:, gcol:gcol + 16].unsqueeze(2).to_broadcast([128, 16, n]), op=ALU.mult),
                ["PT", "vecT"], [dst_key])

        def mm_feat(wbuf, wkey, m, rhsT, t0, n, rkey, K=16):
            acc, akey = next_acc()

            def fn(e):
                ins = None
                for k in range(K):
                    ins = e.matmul(acc[:, 0:n], wbuf[:, k, m * 128:(m + 1) * 128], rhsT[:, k, t0:t0 + n],
                                   start=(k == 0), stop=(k == K - 1))
                return ins
            P.add("pe", fn, [wkey, rkey], [akey])
            return acc, akey

        def lockstep(gens):
            gens = [g_ for g_ in gens if g_ is not None]
            while gens:
                for g_ in list(gens):
                    try:
                        next(g_)
                    except StopIteration:
                        gens.remove(g_)

        def drain(gen):
            if gen is not None:
                for _ in gen:
                    pass

        def mm_into(ps_ap, pkeys, wbuf, wkey, m, rhsT, t0, n, rkey):
            def fn(e):
                ins = None
                for k in range(16):
                    ins = e.matmul(ps_ap, wbuf[:, k, m * 128:(m + 1) * 128], rhsT[:, k, t0:t0 + n],
                                   start=(k == 0), stop=(k == 15))
                return ins
            P.add("pe", fn, [wkey, rkey], pkeys)

        PTs = PT[:, :].rearrange("p (a g s) -> p a g s", g=8, s=16)

        def gn_chain_s():
            tmp = tmpA
            cacc = tmp[5][:, 0:128]
            c3 = cacc.rearrange("p (g s) -> p g s", s=16)
            cb = tmp[1][:, 0:128].bitcast(BF)[:, 0:128]
            sqb = tmp[2][:, 0:128].bitcast(BF)[:, 0:128]
            P.add("act", lambda e: e.activation(out=cb, in_=cacc, func=AF.Copy), ["t5a"], ["t1a"])
            P.add("act", lambda e: e.activation(out=sqb, in_=cacc, func=AF.Square), ["t5a"], ["t2a"])
            (m_ps, mkey) = next_acc()
            (q_ps, qkey) = next_acc()
            P.add("pe", lambda e: e.matmul(m_ps[:, 0:128], onesb, cb, start=True, stop=True), ["t1a", "onesb"], [mkey])
            P.add("pe", lambda e: e.matmul(q_ps[:, 0:128], onesb, sqb, start=True, stop=True), ["t2a", "onesb"], [qkey])
            mean = tmp[3][:, 0:128]
            m2 = tmp[4][:, 0:128]
            mean3 = mean.rearrange("p (g s) -> p g s", s=16)
            P.add("act", lambda e: e.activation(out=mean, in_=m_ps[:, 0:128], func=AF.Copy, scale=1.0 / 128),
                  [mkey], ["t3a"])
            P.add("dve", lambda e: e.tensor_tensor(out=m2, in0=mean, in1=mean, op=ALU.mult), ["t3a"], ["t4a"])
            P.add("dve", lambda e: e.scalar_tensor_tensor(out=m2, in0=q_ps[:, 0:128], scalar=1.0 / 128, in1=m2,
                                                          op0=ALU.mult, op1=ALU.subtract), [qkey, "t4a"], ["t4a"])
            P.add("act", lambda e: e.activation(out=m2, in_=m2, func=AF.Ln, bias=EPS), ["t4a"], ["t4a"])
            P.add("act", lambda e: e.activation(out=m2, in_=m2, func=AF.Exp, scale=-0.5), ["t4a"], ["t4a"])
            P.add("dve", lambda e: e.tensor_tensor(out=mean, in0=cacc, in1=mean, op=ALU.subtract),
                  ["t5a", "t3a"], ["t3a"])
            P.add("dve", lambda e: e.tensor_tensor(out=mean, in0=mean, in1=m2, op=ALU.mult), ["t3a", "t4a"], ["t3a"])
            P.add("dve", lambda e: e.tensor_tensor(
                out=mean3, in0=mean3, in1=vecT[:, V_GNG:V_GNG + 8].unsqueeze(2).to_broadcast([128, 8, 16]),
                op=ALU.mult), ["t3a", "vecT"], ["t3a"])
            P.add("dve", lambda e: e.tensor_tensor(
                out=mean3, in0=mean3, in1=vecT[:, V_GNB:V_GNB + 8].unsqueeze(2).to_broadcast([128, 8, 16]),
                op=ALU.add), ["t3a", "vecT"], ["t3a"])
            sgm = tmp[2][:, 0:128]
            P.add("act", lambda e: e.activation(out=sgm, in_=mean, func=AF.Sigmoid), ["t3a"], ["t2a"])
            P.add("dve", lambda e: e.tensor_tensor(
                out=mixT[:, 8:16, 1024:1040], in0=mean3, in1=sgm.rearrange("p (g s) -> p g s", s=16),
                op=ALU.mult), ["t3a", "t2a"], ["mixT"])

        def gn_chain(cacc, ckey, g, n, col0, tmp, sx):
            cb = tmp[1][:, 0:n].bitcast(BF)[:, 0:n]
            sqb = tmp[2][:, 0:n].bitcast(BF)[:, 0:n]
            P.add("act", lambda e: e.activation(out=cb, in_=cacc, func=AF.Copy), [ckey], ["t1" + sx])
            yield
            P.add("act", lambda e: e.activation(out=sqb, in_=cacc, func=AF.Square), [ckey], ["t2" + sx])
            yield
            (m_ps, mkey) = next_acc()
            (q_ps, qkey) = next_acc()
            P.add("pe", lambda e: e.matmul(m_ps[:, 0:n], onesb, cb, start=True, stop=True), ["t1" + sx, "onesb"], [mkey])
            yield
            P.add("pe", lambda e: e.matmul(q_ps[:, 0:n], onesb, sqb, start=True, stop=True), ["t2" + sx, "onesb"], [qkey])
            yield
            mean = tmp[3][:, 0:n]
            P.add("act", lambda e: e.activation(out=mean, in_=m_ps[:, 0:n], func=AF.Copy, scale=1.0 / 128),
                  [mkey], ["t3" + sx])
            yield
            m2 = tmp[4][:, 0:n]
            P.add("dve", lambda e: e.tensor_tensor(out=m2, in0=mean, in1=mean, op=ALU.mult), ["t3" + sx], ["t4" + sx])
            yield
            P.add("dve", lambda e: e.scalar_tensor_tensor(out=m2, in0=q_ps[:, 0:n], scalar=1.0 / 128, in1=m2,
                                                          op0=ALU.mult, op1=ALU.subtract), [qkey, "t4" + sx], ["t4" + sx])
            yield
            P.add("act", lambda e: e.activation(out=m2, in_=m2, func=AF.Ln, bias=EPS), ["t4" + sx], ["t4" + sx])
            yield
            P.add("act", lambda e: e.activation(out=m2, in_=m2, func=AF.Exp, scale=-0.5), ["t4" + sx], ["t4" + sx])
            yield
            P.add("dve", lambda e: e.tensor_tensor(out=mean, in0=cacc, in1=mean, op=ALU.subtract),
                  [ckey, "t3" + sx], ["t3" + sx])
            yield
            P.add("dve", lambda e: e.tensor_tensor(out=mean, in0=mean, in1=m2, op=ALU.mult), ["t3" + sx, "t4" + sx], ["t3" + sx])
            yield
            P.add("dve", lambda e: e.tensor_scalar(out=mean, in0=mean, scalar1=vecT[:, V_GNG + g:V_GNG + g + 1],
                                                   scalar2=vecT[:, V_GNB + g:V_GNB + g + 1], op0=ALU.mult,
                                                   op1=ALU.add), ["t3" + sx, "vecT"], ["t3" + sx])
            yield
            sgm = tmp[2][:, 0:n]
            P.add("act", lambda e: e.activation(out=sgm, in_=mean, func=AF.Sigmoid), ["t3" + sx], ["t2" + sx])
            yield
            P.add("dve", lambda e: e.tensor_tensor(out=mixT[:, 8 + g, col0:col0 + n], in0=mean, in1=sgm,
                                                   op=ALU.mult), ["t3" + sx, "t2" + sx], ["mixT"])
            yield

        def onorm_a(n):
            PB3 = PB[:, :].rearrange("p (h t) -> p h t", t=128)
            osq3 = osq.rearrange("p (h t) -> p h t", t=128)
            o3 = big[1].rearrange("p (h t) -> p h t", t=128)

            def ocp(e):
                ins = None
                for h in range(8):
                    ins = e.activation(out=o3[:, h, 0:n], in_=PB3[:, h, 0:n], func=AF.Copy,
                                       scale=vecT[:, V_HG + h:V_HG + h + 1])
                return ins
            P.add("act", ocp, ["PB0", "PB1", "vecT"], ["big1"])
            P.add("act", lambda e: e.activation(out=osq3[:, :, 0:n], in_=PB3[:, :, 0:n], func=AF.Square),
                  ["PB0", "PB1"], ["osq"])

        def onorm_b(n, col0):
            PT4 = PA[:, :].rearrange("p (h t) -> p h t", t=128)
            osq3 = osq.rearrange("p (h t) -> p h t", t=128)
            r3 = big[0].rearrange("p (h t) -> p h t", t=128)
            o3 = big[1].rearrange("p (h t) -> p h t", t=128)

            def fn(e):
                ins = None
                for h in range(8):
                    ins = e.matmul(PT4[:, h, 0:n], onesb, osq3[:, h, 0:n], start=True, stop=True)
                return ins
            P.add("pe", fn, ["osq", "onesb"], ["PA0", "PA1"])
            P.add("act", lambda e: e.activation(out=r3[:, :, 0:n], in_=PT4[:, :, 0:n], func=AF.Ln,
                                                scale=1.0 / 128, bias=EPS), ["PA0", "PA1"], ["big0"])
            P.add("act", lambda e: e.activation(out=r3[:, :, 0:n], in_=r3[:, :, 0:n], func=AF.Exp, scale=-0.5),
                  ["big0"], ["big0"])
            P.add("pool", lambda e: e.tensor_tensor(out=o3[:, :, 0:n], in0=o3[:, :, 0:n], in1=r3[:, :, 0:n],
                                                    op=ALU.mult), ["big1", "big0"], ["big1"])
            P.add("pool", lambda e: e.tensor_tensor(out=mixT[:, 0:8, col0:col0 + n], in0=o3[:, :, 0:n],
                                                    in1=mixT[:, 0:8, col0:col0 + n], op=ALU.mult),
                  ["big1", "mixT"], ["mixT"])

        def onorm_chain(n, col0):
            onorm_a(n)
            onorm_b(n, col0)

        PSR = {"PA": (PA, ["PA0", "PA1"]), "PB": (PB, ["PB0", "PB1"]), "PC": (PC, ["PC0", "PC1"]),
               "PT": (PT, ["PT"])}

        def ds_matmul(tile_i, r0, rn, reg):
            ps3 = PSR[reg][0][:, :].rearrange("p (h v) -> p h v", v=128)

            def fn(e):
                ins = None
                for h in range(8):
                    ins = e.matmul(ps3[:, h, :], Kd[r0:r0 + rn, tile_i, h * 128:(h + 1) * 128],
                                   Vt[r0:r0 + rn, tile_i, h * 128:(h + 1) * 128], start=True, stop=True)
                return ins
            P.add("pe", fn, ["RK", "RV"], PSR[reg][1])

        def s_apply(chunk_idx, reg, want_bf):
            ps3 = PSR[reg][0][:, :].rearrange("p (h v) -> p h v", v=128)
            def supd(e):
                ins = None
                for h in range(8):
                    ins = e.scalar_tensor_tensor(out=Sst3[:, h, :], in0=Sst3[:, h, :],
                                                 scalar=ebuf[:, h, chunk_idx:chunk_idx + 1], in1=ps3[:, h, :],
                                                 op0=ALU.mult, op1=ALU.add)
                return ins
            P.add("dve", supd, ["S", "ebuf"] + PSR[reg][1], ["S"])
            if want_bf:
                P.add("dve", lambda e: e.tensor_copy(out=Sbf, in_=Sst), ["S"], ["Sbf"])

        def mixer_pass(row0, blocks, is_main, has_samples, tail_gagb, mix_col0, hook=None, save_tail=False,
                       use_tail=False):
            nprompt = sum(n for _, n in blocks)
            ntot = nprompt + (16 if has_samples else 0)
            tiles = []
            t = 0
            while t < ntot:
                n = min(128, ntot - t)
                tiles.append((t, n))
                t += n
            for (t0, n) in tiles:
                prep_tile(xall[row0 + t0:row0 + t0 + n, :], n, xT, t0, V_GMIX)
            mblocks = list(blocks) + ([(512, 16)] if has_samples else [])
            if save_tail:
                P.add("act", lambda e: e.activation(out=xT_tail, in_=xT[:, :, nprompt - 30:nprompt], func=AF.Copy),
                      ["xT"], ["xTtail"])
            if hook is not None:
                hook()

            if is_main or tail_gagb:
                cblocks = list(blocks) if is_main else [(nprompt - 30, 30)]
                chains = []
                built = set()

                def conv_front(wa, ka, wg, kg_, m, g, t0, n, ch, tail=False):
                    is_s = has_samples and t0 == 512
                    src, skey = (xT_tail, "xTtail") if tail else (xT, "xT")
                    a_ps, akey = mm_feat(wa, ka, m, src, t0, n, skey)
                    b_ps, bkey = mm_feat(wg, kg_, m, src, t0, n, skey)
                    if tail:
                        sg, sgk = sgt[:, 0:n], "scm"
                    else:
                        tmp, sx = next_tset()
                        ch["tmp"], ch["sx"] = tmp, sx
                        sg, sgk = tmp[0][:, 0:n], "t0" + sx
                    P.add("act", lambda e: e.activation(out=sg, in_=b_ps[:, 0:n], func=AF.Sigmoid),
                          [bkey], [sgk])
                    yield
                    if tail:
                        ucol = 0
                    elif is_main:
                        ucol = 30 + t0
                    else:
                        ucol = 30 + nprompt - 30
                    P.add("dve", lambda e: e.tensor_tensor(
                        out=ubuf[:, g, ucol:ucol + n], in0=a_ps[:, 0:n], in1=sg, op=ALU.mult),
                        [akey, sgk], ["u%d" % g])
                    yield
                    if is_main and not is_s and not tail and g not in built:
                        built.add(g)
                        P.add("dve", lambda e: e.tensor_tensor(
                            out=Dg[g % 2], in0=identb.unsqueeze(1).to_broadcast([128, 31, 128]),
                            in1=vecT[:, V_CW + g * 31:V_CW + g * 31 + 31].unsqueeze(2).to_broadcast([128, 31, 128]),
                            op=ALU.mult), ["identb", "vecT"], DK[g % 2])
                        yield

                def conv_mid(m, g, t0, n, ch):
                    if not is_main:
                        return
                    is_s = has_samples and t0 == 512
                    tmp, sx = ch["tmp"], ch["sx"]
                    cacc = tmp[5][:, 0:n]
                    ch["cacc"] = cacc
                    if not is_s:
                        (cps, ckey) = next_acc()

                        def cvf(e):
                            ins = None
                            for j in range(31):
                                ins = e.matmul(cps[:, 0:n], Dg[g % 2][:, j, :], ubuf[:, g, t0 + j:t0 + j + n],
                                               start=(j == 0), stop=(j == 30))
                            return ins
                        P.add("pe", cvf, DK[g % 2] + ["u%d" % g], [ckey])
                        yield
                        P.add("act", lambda e: e.activation(out=cacc, in_=cps[:, 0:n], func=AF.Identity,
                                                            bias=vecT[:, V_CB + g:V_CB + g + 1]),
                              [ckey, "vecT"], ["t5" + sx])
                        yield
                    else:
                        (cs_ps, cskey) = next_acc()

                        def csf(e):
                            ins = None
                            for tt in range(4):
                                ins = e.matmul(cs_ps[:, 0:16], prodb[tt][0:120, g * 128:(g + 1) * 128],
                                               indb3[0:120, tt, :], start=(tt == 0), stop=(tt == 3))
                            return ins
                        P.add("pe", csf, ["RV", "indb"], [cskey])
                        yield
                        P.add("dve", lambda e: e.scalar_tensor_tensor(
                            out=cacc, in0=ubuf[:, g, 542:558],
                            scalar=vecT[:, V_CW + g * 31 + 30:V_CW + g * 31 + 31],
                            in1=cs_ps[:, 0:16], op0=ALU.mult, op1=ALU.add),
                            ["u%d" % g, cskey, "vecT"], ["t5" + sx])
                        yield
                        P.add("dve", lambda e: e.tensor_scalar(
                            out=cacc, in0=cacc, scalar1=vecT[:, V_CB + g:V_CB + g + 1], scalar2=None,
                            op0=ALU.add), ["t5" + sx, "vecT"], ["t5" + sx])
                        yield

                def conv_back(m, g, t0, n, ch):
                    if not is_main:
                        return
                    is_s = has_samples and t0 == 512
                    col = (mix_col0 + t0) if not is_s else 1024
                    yield from gn_chain(ch["cacc"], "t5" + ch["sx"], g, n, col, ch["tmp"], ch["sx"])

                def pump(front_gen, final=False):
                    i = len(chains) - 1
                    if final:
                        i += 1
                    gens = [front_gen]
                    for (stage, k) in (("mid", i - 1), ("back", i - 2)):
                        if 0 <= k < len(chains) and stage not in chains[k]["done"]:
                            c_ = chains[k]
                            chains[k]["done"].add(stage)
                            gens.append((conv_mid if stage == "mid" else conv_back)(
                                c_["m"], c_["g"], c_["t0"], c_["n"], c_))
                    lockstep(gens)

                for gp in range(4):
                    wa, ka = stream_w(w_in_v, 4096 + gp * 256)
                    wg, kg_ = stream_w(w_in_v, 5120 + gp * 256)
                    for m in range(2):
                        g = gp * 2 + m
                        if use_tail:
                            drain(conv_front(wa, ka, wg, kg_, m, g, 0, 30, {}, tail=True))
                        for (t0, n) in cblocks:
                            ch = {"m": m, "g": g, "t0": t0, "n": n, "done": set()}
                            chains.append(ch)
                            pump(conv_front(wa, ka, wg, kg_, m, g, t0, n, ch))
                        if has_samples:
                            mm_into(PTs[:, 0, g, :], ["PT"], wa, ka, m, xT, 512, 16, "xT")
                            mm_into(PTs[:, 4, g, :], ["PT"], wg, kg_, m, xT, 512, 16, "xT")
                    if is_main and has_samples:
                        for (r0, rn, okey) in ((482, 30, "ctail"), (512, 16, "scn29")):
                            (ta, tka) = next_acc()
                            (tb, tkb) = next_acc()

                            def tma(e, ta=ta, r0=r0, rn=rn, wa=wa):
                                ins = None
                                for k in range(16):
                                    ins = e.matmul(ta[0:rn, 0:256], xT[:, k, r0:r0 + rn], wa[:, k, 0:256],
                                                   start=(k == 0), stop=(k == 15))
                                return ins

                            def tmb(e, tb=tb, r0=r0, rn=rn, wg=wg):
                                ins = None
                                for k in range(16):
                                    ins = e.matmul(tb[0:rn, 0:256], xT[:, k, r0:r0 + rn], wg[:, k, 0:256],
                                                   start=(k == 0), stop=(k == 15))
                                return ins
                            P.add("pe", tma, [ka, "xT"], [tka])
                            P.add("pe", tmb, [kg_, "xT"], [tkb])
                            P.add("act", lambda e, tb=tb, rn=rn: e.activation(
                                out=sgt[0:rn, :], in_=tb[0:rn, 0:256], func=AF.Sigmoid), [tkb], ["scm"])
                            urow = utok if okey == "ctail" else utok2
                            P.add("dve", lambda e, ta=ta, rn=rn, gp=gp, urow=urow: e.tensor_tensor(
                                out=urow[0:rn, gp * 256:(gp + 1) * 256], in0=ta[0:rn, 0:256], in1=sgt[0:rn, :],
                                op=ALU.mult), [tka, "scm"], ["big0" if okey == "ctail" else "big1"])
                pump(None, final=True)
                for k_ in range(len(chains)):
                    for stage in ("mid", "back"):
                        if stage not in chains[k_]["done"]:
                            chains[k_]["done"].add(stage)
                            c_ = chains[k_]
                            drain((conv_mid if stage == "mid" else conv_back)(c_["m"], c_["g"], c_["t0"], c_["n"], c_))
                if is_main and has_samples:
                    sgS_ = tmpA[0][:, 0:128]
                    P.add("act", lambda e: e.activation(out=sgS_, in_=PT[:, 512:640], func=AF.Sigmoid),
                          ["PT"], ["t0a"])
                    P.add("dve", lambda e: e.tensor_tensor(
                        out=ubuf[:, :, 542:558], in0=PTs[:, 0, :, :],
                        in1=sgS_.rearrange("p (g s) -> p g s", s=16), op=ALU.mult),
                        ["PT", "t0a"], ["u%d" % g for g in range(8)])

                    def csf_all(e):
                        ins = None
                        for g in range(8):
                            for tt in range(4):
                                ins = e.matmul(PTs[:, 1, g, :], prodb[tt][0:120, g * 128:(g + 1) * 128],
                                               indb3[0:120, tt, :], start=(tt == 0), stop=(tt == 3))
                        return ins
                    P.add("pe", csf_all, ["RV", "indb", "u0"], ["PT"])
                    cS3 = tmpA[5][:, 0:128].rearrange("p (g s) -> p g s", s=16)
                    P.add("dve", lambda e: e.tensor_tensor(
                        out=cS3, in0=ubuf[:, :, 542:558],
                        in1=vecT[:, V_W30:V_W30 + 8].unsqueeze(2).to_broadcast([128, 8, 16]), op=ALU.mult),
                        ["u%d" % g for g in range(8)] + ["vecT"], ["t5a"])
                    P.add("dve", lambda e: e.tensor_tensor(out=cS3, in0=cS3, in1=PTs[:, 1, :, :], op=ALU.add),
                          ["t5a", "PT"], ["t5a"])
                    P.add("dve", lambda e: e.tensor_tensor(
                        out=cS3, in0=cS3, in1=vecT[:, V_CB:V_CB + 8].unsqueeze(2).to_broadcast([128, 8, 16]),
                        op=ALU.add), ["t5a", "vecT"], ["t5a"])
                    gn_chain_s()
                    P.dma("sp", [(ctail_d, utok[0:30, :])], reads=["big0"], key="st_utok")
                    P.dma("sp", [(scn_d[:, 29, :], utok2[0:16, :])], reads=["big1"], key="st_utok2")

            if is_main:
                for (cbase, dst, dcol0) in ((3072, mixT, mix_col0), (0, Qb, 0)):
                    for gp in range(4):
                        w, wk = stream_w(w_in_v, cbase + gp * 256)
                        for m in range(2):
                            h = gp * 2 + m
                            for (t0, n) in mblocks:
                                ps, pk = mm_feat(w, wk, m, xT, t0, n, "xT")
                                if dst is mixT:
                                    c0_ = (1024 if (has_samples and t0 == 512) else mix_col0 + t0)
                                    dkey = "mixT"
                                else:
                                    c0_ = t0
                                    dkey = "Qb%d" % h
                                P.add("act", lambda e, ps=ps, dst=dst, h=h, c0_=c0_, n=n: e.activation(
                                    out=dst[:, h, c0_:c0_ + n], in_=ps[:, 0:n], func=AF.Silu), [pk], [dkey])

            chunk_of = {}
            nch = 0
            for (t0, n) in blocks:
                for c in range(0, n, 64):
                    chunk_of[t0 + c] = nch
                    nch += 1
            def f_chain(w, wk, m, h, t0, n, is_s):
                ps, pk = mm_feat(w, wk, m, xT, t0, n, "xT")
                tmp, sx = next_tset()
                fg = tmp[0][:, 0:n]
                lf = tmp[1][:, 0:n]
                kgt = tmp[2][:, 0:n]
                bb = tmp[3][:, 0:n]
                rr = tmp[4][:, 0:n]
                dd = tmp[5][:, 0:n]
                yield
                P.add("act", lambda e: e.activation(out=fg, in_=ps[:, 0:n], func=AF.Sigmoid), [pk], ["t0" + sx])
                yield
                P.add("dve", lambda e: e.tensor_scalar(
                    out=fg, in0=fg, scalar1=oml[:, h:h + 1], scalar2=lbv[:, h:h + 1], op0=ALU.mult,
                    op1=ALU.add), ["t0" + sx, "oml", "lbv"], ["t0" + sx])
                yield
                if is_s:
                    P.add("act", lambda e: e.activation(out=fgS[:, h, :], in_=fg, func=AF.Copy),
                          ["t0" + sx], ["fgS"])
                    yield
                    P.add("act", lambda e: e.activation(out=kgS[:, h, :], in_=fg, func=AF.Identity, scale=-1.0,
                                                        bias=1.0), ["t0" + sx], ["kgS"])
                    yield
                    return
                P.add("act", lambda e: e.activation(out=lf, in_=fg, func=AF.Ln), ["t0" + sx], ["t1" + sx])
                yield
                P.add("act", lambda e: e.activation(out=kgt, in_=fg, func=AF.Identity, scale=-1.0, bias=1.0),
                      ["t0" + sx], ["t2" + sx])
                P.add("dve", lambda e: e.tensor_tensor_scan(
                    out=bb, data0=rmask[:, 0:n], data1=lf, initial=0.0, op0=ALU.mult, op1=ALU.add),
                    ["t1" + sx, "cst"], ["t3" + sx])
                yield
                cw = min(64, n)
                ncb = n // cw
                bb3 = bb.rearrange("p (c t) -> p c t", t=cw)
                rr3 = rr.rearrange("p (c t) -> p c t", t=cw)
                dd3 = dd.rearrange("p (c t) -> p c t", t=cw)
                P.add("dve", lambda e: e.tensor_tensor(
                    out=rr3, in0=bb3[:, :, cw - 1:cw].to_broadcast([128, ncb, cw]), in1=bb3,
                    op=ALU.subtract), ["t3" + sx], ["t4" + sx])
                c_first = chunk_of[t0]
                P.add("act", lambda e: e.activation(
                    out=ebuf[:, h, c_first:c_first + ncb], in_=bb3[:, :, cw - 1], func=AF.Exp),
                    ["t3" + sx], ["ebuf"])
                yield
                if is_main:
                    mid = cw // 2 - 1
                    P.add("dve", lambda e: e.tensor_tensor(
                        out=dd3, in0=bb3, in1=bb3[:, :, mid:mid + 1].to_broadcast([128, ncb, cw]),
                        op=ALU.subtract), ["t3" + sx], ["t5" + sx])
                P.add("act", lambda e: e.activation(out=rr, in_=rr, func=AF.Exp), ["t4" + sx], ["t4" + sx])
                yield
                P.add("dve", lambda e: e.tensor_tensor(
                    out=KdT[:, h, t0:t0 + n], in0=kgt, in1=rr, op=ALU.mult), ["t4" + sx, "t2" + sx], ["KdT%d" % h])
                yield
                if is_main:
                    P.add("act", lambda e: e.activation(out=rr, in_=dd, func=AF.Exp), ["t5" + sx], ["t4" + sx])
                    P.add("act", lambda e: e.activation(out=dd, in_=dd, func=AF.Exp, scale=-1.0),
                          ["t5" + sx], ["t5" + sx])
                    P.add("act", lambda e: e.activation(out=bb, in_=bb, func=AF.Exp), ["t3" + sx], ["t3" + sx])
                    yield
                    P.add("dve", lambda e: e.tensor_tensor(
                        out=Qt[:, h, t0:t0 + n], in0=Qb[:, h, t0:t0 + n], in1=rr, op=ALU.mult),
                        ["t4" + sx, "Qb%d" % h], ["Qt%d" % h])
                    yield
                    P.add("dve", lambda e: e.tensor_tensor(
                        out=Kt[:, h, t0:t0 + n], in0=kgt, in1=dd, op=ALU.mult), ["t5" + sx, "t2" + sx], ["Kt%d" % h])
                    yield
                    P.add("dve", lambda e: e.tensor_tensor(
                        out=Qb[:, h, t0:t0 + n], in0=Qb[:, h, t0:t0 + n], in1=bb, op=ALU.mult),
                        ["t3" + sx, "Qb%d" % h, "Qt%d" % h], ["Qb%d" % h])
                    yield

            def i_gen(w, wk, cb):
                for ti, (t0, n) in enumerate(tiles):
                    (ps, pk) = next_acc()

                    def vfn(e, ps=ps, t0=t0, n=n):
                        ins = None
                        for k in range(16):
                            ins = e.matmul(ps[0:n, 0:256], xT[:, k, t0:t0 + n], w[:, k, 0:256],
                                           start=(k == 0), stop=(k == 15))
                        return ins
                    P.add("pe", vfn, [wk, "xT"], [pk])
                    yield
                    P.add("act", lambda e, ps=ps, ti=ti, n=n: e.activation(
                        out=Vt[0:n, ti, cb * 256:(cb + 1) * 256], in_=ps[0:n, 0:256], func=AF.Copy), [pk], ["RV"])
                    yield
                    yield

            for gp in range(4):
                w, wk = stream_w(w_in_v, 1024 + gp * 256)
                wi, wik = stream_w(w_in_v, 2048 + gp * 256)
                pblocks = [(t0, n) for (t0, n) in mblocks if not (has_samples and t0 == 512)]
                igen = i_gen(wi, wik, gp)
                for (t0, n) in pblocks:
                    lockstep([f_chain(w, wk, m, gp * 2 + m, t0, n, False) for m in range(2)] + [igen])
                    igen = None
                if has_samples:
                    for m in range(2):
                        mm_into(PTs[:, 0, gp * 2 + m, :], ["PT"], w, wk, m, xT, 512, 16, "xT")

            if has_samples:
                fg2 = fgS.rearrange("p h s -> p (h s)")
                P.add("act", lambda e: e.activation(out=fg2, in_=PT[:, 0:128], func=AF.Sigmoid), ["PT"], ["fgS"])
                P.add("dve", lambda e: e.tensor_tensor(
                    out=fgS, in0=fgS, in1=oml[:, 0:8].unsqueeze(2).to_broadcast([128, 8, 16]), op=ALU.mult),
                    ["fgS", "oml"], ["fgS"])
                P.add("dve", lambda e: e.tensor_tensor(
                    out=fgS, in0=fgS, in1=lbv[:, 0:8].unsqueeze(2).to_broadcast([128, 8, 16]), op=ALU.add),
                    ["fgS", "lbv"], ["fgS"])
                P.add("act", lambda e: e.activation(out=kgS.rearrange("p h s -> p (h s)"), in_=fg2,
                                                    func=AF.Identity, scale=-1.0, bias=1.0), ["fgS"], ["kgS"])

            ptiles = [(t0, n) for (t0, n) in tiles if t0 < nprompt]
            PTk = PTb[:, 0:1024].rearrange("p (h k) -> p h k", k=128)
            for ti, (t0, n) in enumerate(ptiles):
                def ktr(e, t0=t0, n=n):
                    ins = None
                    for h in range(8):
                        ins = e.transpose(out=PTk[0:n, h, :], in_=KdT[:, h, t0:t0 + n], identity=identb)
                    return ins
                P.add("pe", ktr, ["KdT%d" % h for h in range(8)] + ["identb"], ["PT"])
                P.add("dve", lambda e, ti=ti, n=n: e.tensor_copy(out=Kd[0:n, ti, :], in_=PTb[0:n, 0:1024]),
                      ["PT"], ["RK"])

            PA3 = PA[:, :].rearrange("p (h t) -> p h t", t=128)
            PB3 = PB[:, :].rearrange("p (h t) -> p h t", t=128)
            PT4 = PT[:, :].rearrange("p (h t) -> p h t", t=128)
            if not is_main:
                allch = []
                for ti, (t0, n) in enumerate(ptiles):
                    for c in range(0, n, 64):
                        allch.append((ti, c, min(64, n - c), chunk_of[t0 + c]))
                regs = ["PA", "PB", "PC"]
                for i, (ti, c, cn, cidx) in enumerate(allch):
                    ds_matmul(ti, c, cn, regs[i % 3])
                    s_apply(cidx, regs[i % 3], i == len(allch) - 1)
            else:
                pend_b = []
                for ti, (t0, n) in enumerate(ptiles):
                    chunks = [(c, min(64, n - c)) for c in range(0, n, 64)]

                    def scf(e, t0=t0, n=n):
                        ins = None
                        for h in range(8):
                            ins = e.matmul(PA3[0:n, h, 0:n], Kt[:, h, t0:t0 + n], Qt[:, h, t0:t0 + n],
                                           start=True, stop=True)
                        return ins
                    P.add("pe", scf, ["Kt%d" % h for h in range(8)] + ["Qt%d" % h for h in range(8)], ["PA0", "PA1"])
                    P.add("dve", lambda e, n=n: e.tensor_tensor(
                        out=scm[0:n, :, 0:n], in0=PA3[0:n, :, 0:n],
                        in1=mask2[0:n, 0:n].unsqueeze(1).to_broadcast([n, 8, n]), op=ALU.mult),
                        ["PA0", "PA1", "cst"], ["scm"])

                    def of1(e, ti=ti, n=n):
                        ins = None
                        for h in range(8):
                            ins = e.matmul(PB3[:, h, 0:n], Vt[0:n, ti, h * 128:(h + 1) * 128], scm[0:n, h, 0:n],
                                           start=(h % 4 == 0), stop=False, skip_group_check=True)
                        return ins
                    P.add("pe", of1, ["RV", "scm"], ["PB0", "PB1"])
                    dregs = ["PC", "PT"]
                    for ci, (c, cn) in enumerate(chunks):
                        ds_matmul(ti, c, cn, dregs[ci])
                    for ci, (c, cn) in enumerate(chunks):
                        def of2(e, t0=t0, c=c, cn=cn):
                            ins = None
                            for h in range(8):
                                ins = e.matmul(PB3[:, h, c:c + cn], Sbf3[:, h, :], Qb[:, h, t0 + c:t0 + c + cn],
                                               start=False, stop=True, skip_group_check=True)
                            return ins
                        P.add("pe", of2, ["Sbf"] + ["Qb%d" % h for h in range(8)], ["PB0", "PB1"])
                        s_apply(chunk_of[t0 + c], dregs[ci], True)
                    if pend_b:
                        pend_b.pop(0)()
                    onorm_a(n)
                    pend_b.append(lambda n=n, col=mix_col0 + t0: onorm_b(n, col))
                while pend_b:
                    pend_b.pop(0)()

            if is_main or tail_gagb:
                for g in range(8):
                    P.add("dve", lambda e, g=g: e.tensor_copy(out=ubuf[:, g, 0:30],
                                                              in_=ubuf[:, g, nprompt:nprompt + 30]),
                          ["u%d" % g], ["u%d" % g])

        P.add("dve", lambda e: e.memset(ubuf, 0.0), [], ["u%d" % g for g in range(8)])

        mixer_pass(0, [(0, 512)], False, False, False, 0)
        mixer_pass(512, [(0, 512), (512, 16)], False, False, False, 0, save_tail=True)
        mixer_pass(1040, [(0, 512)], True, False, False, 0, use_tail=True)

        P.dma("sp", [(scn_d[:, 0:29, :], sc[:, 1:30, :])], key="scn_copy", writes=["scn_rows"])

        def sample_hook():
            P.dma("sp", [(wrep[0:120, :], wrep_d)], writes=["RK"], key="ld_wrep")
            sc_rows = sc.rearrange("s j c -> (s j) c")
            for tt in range(4):
                P.dma("sp", [(scin1[0:120, :], sc_rows[tt * 120:(tt + 1) * 120, :])], writes=["RK"], key="ld_scin")
                P.add("dve", lambda e, tt=tt: e.tensor_tensor(out=prodb[tt][0:120, :], in0=scin1[0:120, :],
                                                             in1=wrep[0:120, :], op=ALU.mult),
                      ["RK"], ["RV"])

        mixer_pass(1552, [(0, 512)], True, True, False, 512, hook=sample_hook)

        wo_pre = [arena[:, scratch_mark + i * 4096:scratch_mark + (i + 1) * 4096].bitcast(BF).rearrange(
            "p (k n) -> p k n", n=512) for i in range(2)]
        for i in range(2):
            P.dma("pool", [(wo_pre[i], w_out_v[:, :, i * 512:(i + 1) * 512])],
                  writes=["wb%d" % (2 * i), "wb%d" % (2 * i + 1)], key="wo%d" % i)

        P.dma("sp", [(sout_d.rearrange("h k v -> k h v"), Sst3)], reads=["S"], key="st_S")

        PA3 = PA[:, :].rearrange("p (h t) -> p h t", t=128)
        PB3 = PB[:, :].rearrange("p (h t) -> p h t", t=128)
        P.add("dve", lambda e: e.tensor_copy(
            out=onehot3[0:16], in_=cst[0:16, C_ID:C_ID + 16].unsqueeze(2).to_broadcast([16, 16, 128])),
            ["cst"], ["KdT%d" % h for h in range(8)])
        ALIAS = [["Qt%d" % h for h in range(8)], ["Qt%d" % h for h in range(8)],
                 ["KdT%d" % h for h in range(8)], ["Kt%d" % h for h in range(8)]]
        PC3s = PC[:, :].rearrange("p (h t) -> p h t", t=128)
        VB = [(PA, PA3, ["PA0", "PA1"]), (PC, PC3s, ["PC0", "PC1"])]

        def vb_mm(s):
            (pt_, _, keys_) = VB[s % 2]

            def vbf(e):
                e.matmul(pt_[:, 0:512], onehot3[0:16, s, :], Vt[0:16, 4, 0:512], start=True, stop=True)
                return e.matmul(pt_[:, 512:1024], onehot3[0:16, s, :], Vt[0:16, 4, 512:1024], start=True, stop=True)
            P.add("pe", vbf, ["KdT0", "RV"], keys_)

        for s in range(4):
            bi = s % 4
            P.dma("sp", [(S0b[bi].rearrange("p (h v) -> p h v", v=128), sh[s].rearrange("h k v -> k h v"))],
                  writes=["S0_%d" % bi] + ALIAS[bi], key="ld_S0_%d" % bi)
        vb_mm(0)
        for s in range(16):
            bi = s % 4
            S0 = S0b[bi]
            k0 = "S0_%d" % bi
            S03 = S0.rearrange("p (h v) -> p h v", v=128)
            t3 = big[s % 2].rearrange("p (h v) -> p h v", v=128)
            tk = "big%d" % (s % 2)
            P.add("pool", lambda e, s=s, S03=S03: e.tensor_tensor(
                out=S03, in0=S03, in1=fgS[:, :, s:s + 1].to_broadcast([128, 8, 128]), op=ALU.mult),
                [k0, "fgS"], [k0])

            if s + 1 < 16:
                vb_mm(s + 1)

            def sfu(e, s=s, S03=S03, vb3=VB[s % 2][1]):
                ins = None
                for h in range(8):
                    ins = e.scalar_tensor_tensor(out=S03[:, h, :], in0=vb3[:, h, :], scalar=kgS[:, h, s:s + 1],
                                                 in1=S03[:, h, :], op0=ALU.mult, op1=ALU.add)
                return ins
            P.add("dve", sfu, VB[s % 2][2] + ["kgS", k0], [k0])
            Snb = Snbs[s % 2]
            sk = "Snb%d" % (s % 2)
            P.add("act", lambda e, S0=S0, Snb=Snb: e.activation(out=Snb, in_=S0, func=AF.Copy), [k0],
                  [sk] + (["Kt%d" % h for h in range(8)] if s < 2 else []))
            Snb3 = Snb.rearrange("p (h v) -> p h v", v=128)

            def osf(e, s=s, Snb3=Snb3):
                ins = None
                for h in range(8):
                    ins = e.matmul(PB3[:, h, s:s + 1], Snb3[:, h, :], Qb[:, h, 512 + s:513 + s],
                                   start=True, stop=True)
                return ins
            P.add("pe", osf, [sk] + ["Qb%d" % h for h in range(8)], ["PB0", "PB1"])
            P.dma("act", [(shn_d[s].rearrange("h k v -> k h v"), S03)], reads=[k0], key="st_" + k0)
            if s + 4 < 16:
                P.dma("sp", [(S03, sh[s + 4].rearrange("h k v -> k h v"))], writes=[k0], key="ld_" + k0)
        onorm_chain(16, 1024)

        P.barrier()
        prep_scale_on_act[0] = False
        reset(scratch_mark)
        hfT = A1.rearrange("p (k t) -> p k t", t=1040)
        wo = [alloc(4096, BF).rearrange("p (k n) -> p k n", n=512) for _ in range(4)]
        xin2 = [alloc(2048) for _ in range(2)]
        hti = [alloc(2048) for _ in range(2)]
        xnb_c = alloc(1024, BF)
        junk_c = alloc(1024, BF)
        alloc(352)
        wD01 = [alloc(2048, BF).rearrange("p (k n) -> p k n", n=256) for _ in range(2)]
        for cbk in range(2, 4):
            P.dma("pool", [(wo[cbk], w_out_v[:, :, cbk * 512:(cbk + 1) * 512])], writes=["wo%d" % cbk],
                  key="wo%d" % cbk)
        tilesC = [(i * 128, 128) for i in range(8)] + [(1024, 16)]
        deferred = []

        def c_mm(ti, cbk):
            t0, n = tilesC[ti]
            xt = xin2[ti % 2]
            kx = "xc%d" % (ti % 2)
            ht = hti[ti % 2]
            kh = "ht%d" % (ti % 2)
            (ps, pk) = next_acc()

            def cf(e):
                ins = None
                for k in range(16):
                    ins = e.matmul(ps[0:n, :], mixT[:, k, t0:t0 + n], wo[cbk][:, k, :],
                                   start=(k == 0), stop=(k == 15))
                return ins
            P.add("pe", cf, ["mixT", "wo%d" % cbk], [pk])
            P.add("dve", lambda e: e.tensor_tensor(
                out=ht[0:n, cbk * 512:(cbk + 1) * 512], in0=ps[0:n, :], in1=xt[0:n, cbk * 512:(cbk + 1) * 512],
                op=ALU.add), [pk, kx], [kh])

        def c_load(ti):
            t0, n = tilesC[ti]
            P.dma("sp", [(xin2[ti % 2][0:n, :], xall[1040 + t0:1040 + t0 + n, :])], writes=["xc%d" % (ti % 2)],
                  key="xc%d" % (ti % 2))

        def c_finish(ti):
            t0, n = tilesC[ti]
            ht = hti[ti % 2]
            kh = "ht%d" % (ti % 2)
            P.dma("sp", [(hscr[t0:t0 + n, :], ht[0:n, :])], reads=[kh], writes=["hscr%d" % ti], key="st_" + kh)
            if deferred:
                deferred.pop(0)()
            deferred.append(lambda: norm_transpose(ht, kh, n, hfT, t0, V_GFFN, "hfT", xnb_c, "xnb_c", junk_c,
                                                   ["junk_c"]))

        c_load(0)
        c_load(1)
        for cbk in range(4):
            c_mm(0, cbk)
            c_mm(1, cbk)
        c_finish(0)
        c_finish(1)
        def c_mm_pe(ti, cbk):
            t0, n = tilesC[ti]
            (ps, pk) = next_acc()

            def cf(e):
                ins = None
                for k in range(16):
                    ins = e.matmul(ps[0:n, :], mixT[:, k, t0:t0 + n], wo[cbk][:, k, :],
                                   start=(k == 0), stop=(k == 15))
                return ins
            P.add("pe", cf, ["mixT", "wo%d" % cbk], [pk])
            return ps, pk

        def c_add(ti, cbk, ps, pk):
            t0, n = tilesC[ti]
            xt = xin2[ti % 2]
            ht = hti[ti % 2]
            P.add("dve", lambda e: e.tensor_tensor(
                out=ht[0:n, cbk * 512:(cbk + 1) * 512], in0=ps[0:n, :], in1=xt[0:n, cbk * 512:(cbk + 1) * 512],
                op=ALU.add), [pk, "xc%d" % (ti % 2)], ["ht%d" % (ti % 2)])

        for ti in range(2, len(tilesC)):
            c_load(ti)
            accs_c = [c_mm_pe(ti, cbk) for cbk in range(4)]
            if deferred:
                deferred.pop(0)()
            for cbk in range(4):
                c_add(ti, cbk, *accs_c[cbk])
            c_finish(ti)
        while deferred:
            deferred.pop(0)()
        P.dma("pool", [(wD01[0], w_gate_v[:, :, 0:256])], writes=["wD0"], key="wD0")
        P.dma("pool", [(wD01[1], w_up_v[:, :, 0:256])], writes=["wD1"], key="wD1")

        P.barrier()
        reset(base_mark)
        sgD = [alloc(512) for _ in range(2)]
        wE0 = alloc(5632, BF).rearrange("p (k n) -> p k n", n=256)
        reset(base_mark + 8320)
        wE1 = alloc(5632, BF).rearrange("p (k n) -> p k n", n=256)
        wE = [wE0, wE1]
        reset(scratch_mark)
        hmid = alloc(NFF * 520, BF).rearrange("p (c t) -> p c t", t=1040)
        e_mark = mark()
        wD23 = [alloc(2048, BF).rearrange("p (k n) -> p k n", n=256) for _ in range(2)]
        wD = [wD01[0], wD01[1], wD23[0], wD23[1]]
        blocksD = [(0, 512), (512, 512), (1024, 16)]
        wd_i = [0]

        def stream_D(view, c0):
            i = wd_i[0] % 4
            first = wd_i[0] < 2
            wd_i[0] += 1
            if not first:
                P.dma("pool", [(wD[i], view[:, :, c0:c0 + 256])], writes=["wD%d" % i], key="wD%d" % i)
            return wD[i], "wD%d" % i
        sg_i = [0]
        for fp in range(NFF // 2):
            wgt, kgt_ = stream_D(w_gate_v, fp * 256)
            wup, kup = stream_D(w_up_v, fp * 256)
            for m in range(2):
                fc = fp * 2 + m
                for (t0, n) in blocksD:
                    g_ps, gk = mm_feat(wgt, kgt_, m, hfT, t0, n, "hfT")
                    u_ps, uk = mm_feat(wup, kup, m, hfT, t0, n, "hfT")
                    sg = sgD[sg_i[0] % 2]
                    sk = "sgD%d" % (sg_i[0] % 2)
                    sg_i[0] += 1
                    P.add("act", lambda e, g_ps=g_ps, sg=sg, n=n: e.activation(out=sg[:, 0:n], in_=g_ps[:, 0:n],
                                                                              func=AF.Silu), [gk], [sk])
                    P.add("dve", lambda e, u_ps=u_ps, sg=sg, fc=fc, t0=t0, n=n: e.tensor_tensor(
                        out=hmid[:, fc, t0:t0 + n], in0=u_ps[:, 0:n], in1=sg[:, 0:n], op=ALU.mult),
                        [uk, sk], ["hmid"])

        P.dma("pool", [(wE[0], w_down_v[:, :, 0:256])], writes=["wE0"], key="wE0")

        P.barrier()
        reset(e_mark)
        hp = [alloc(256) for _ in range(2)]
        h2p = [alloc(256) for _ in range(2)]
        h2f = [alloc(2048) for _ in range(3)]
        gfin = alloc(2048)
        ssE = alloc(9 * 8).rearrange("p (t c) -> p t c", c=8)
        ssF = alloc(16)
        jnk = alloc(128, BF)
        P.dma("sp", [(gfin, gfin_d)], writes=["gfin"], key="gfin")
        it = [0]
        for cb in range(8):
            wb = wE[cb % 2]
            wk = "wE%d" % (cb % 2)
            if cb > 0:
                P.dma("pool", [(wb, w_down_v[:, :, cb * 256:(cb + 1) * 256])], writes=[wk], key=wk)
            for ti, (t0, n) in enumerate(tilesC):
                i = it[0] % 2
                it[0] += 1
                j = ti % 3
                (ps, pk) = next_acc()
                if cb == 7:
                    P.dma("pool", [(h2f[j][0:n, 0:1792], h2scr[t0:t0 + n, 0:1792])], reads=["h2scr%d" % ti],
                          writes=["h2f%d" % j], key="h2f%d" % j)

                def ef(e, ps=ps, t0=t0, n=n, wb=wb):
                    ins = None
                    for k in range(NFF):
                        ins = e.matmul(ps[0:n, 0:256], hmid[:, k, t0:t0 + n], wb[:, k, :],
                                       start=(k == 0), stop=(k == NFF - 1))
                    return ins
                P.add("pe", ef, ["hmid", wk], [pk])
                P.dma("act", [(hp[i][0:n, :], hscr[t0:t0 + n, cb * 256:(cb + 1) * 256])],
                      reads=["hscr%d" % ti], writes=["hp%d" % i], key="hp%d" % i)
                P.add("dve", lambda e, ps=ps, i=i, n=n: e.tensor_tensor(
                    out=h2p[i][0:n, :], in0=ps[0:n, 0:256], in1=hp[i][0:n, :], op=ALU.add),
                    [pk, "hp%d" % i], ["h2p%d" % i])
                P.add("act", lambda e, i=i, n=n, ti=ti, cb=cb: e.activation(
                    out=jnk[0:n, :], in_=h2p[i][0:n, :], func=AF.Square, accum_out=ssE[0:n, ti, cb:cb + 1]),
                    ["h2p%d" % i], ["jnk", "ssE%d" % ti])
                if cb < 7:
                    P.dma("sp", [(h2scr[t0:t0 + n, cb * 256:(cb + 1) * 256], h2p[i][0:n, :])],
                          reads=["h2p%d" % i], writes=["h2scr%d" % ti], key="st_h2p%d" % i)
                else:
                    P.add("dve", lambda e, n=n, ti=ti: e.tensor_reduce(
                        out=ssF[0:n, 0:1], in_=ssE[0:n, ti, :], axis=mybir.AxisListType.X, op=ALU.add),
                        ["ssE%d" % ti], ["ssF"])
                    P.add("act", lambda e, n=n: e.activation(out=ssF[0:n, 1:2], in_=ssF[0:n, 0:1], func=AF.Sqrt,
                                                            scale=1.0 / D, bias=EPS), ["ssF"], ["ssFb"])
                    P.add("dve", lambda e, n=n: e.reciprocal(out=ssF[0:n, 2:3], in_=ssF[0:n, 1:2]),
                          ["ssFb"], ["ssFc"])
                    P.add("dve", lambda e, n=n, j=j, i=i: e.scalar_tensor_tensor(
                        out=h2f[j][0:n, 1792:2048], in0=h2p[i][0:n, :], scalar=ssF[0:n, 2:3],
                        in1=gfin[0:n, 1792:2048], op0=ALU.mult, op1=ALU.mult),
                        ["h2p%d" % i, "ssFc", "gfin", "h2f%d" % j], ["h2f%d" % j])
                    P.add("dve", lambda e, n=n, j=j: e.scalar_tensor_tensor(
                        out=h2f[j][0:n, 0:1792], in0=h2f[j][0:n, 0:1792], scalar=ssF[0:n, 2:3],
                        in1=gfin[0:n, 0:1792], op0=ALU.mult, op1=ALU.mult),
                        ["h2f%d" % j, "ssFc", "gfin"], ["h2f%d" % j])
                    P.dma("sp", [(y_d[t0:t0 + n, :], h2f[j][0:n, :])], reads=["h2f%d" % j],
                          writes=["y%d" % ti], key="st_h2f%d" % j)

        P.wait_all_dma("sp")

        with nc.Block() as block:
            P.emit(block)
    return nc


_CACHE = {}


def _consts():
    cst = np.zeros((128, NCST), np.float32)
    cst[:, C_ID:C_ID + 128] = np.eye(128, dtype=np.float32)
    s = np.arange(128)[:, None]
    t = np.arange(128)[None, :]
    cst[:, C_MASK:C_MASK + 128] = ((s // 64 == t // 64) & (s <= t)).astype(np.float32)
    rm = np.ones((512,), np.float32)
    rm[::64] = 0.0
    cst[:, C_RM:C_RM + 512] = rm[None, :]
    cst[:, C_ONE:C_ONE + 128] = 1.0
    ind = np.zeros((128, 4, 16), np.float32)
    for tt in range(4):
        for sp in range(4):
            ind[sp * 30:(sp + 1) * 30, tt, tt * 4 + sp] = 1.0
    cst[:, C_IND:C_IND + 64] = ind.reshape(128, 64)
    return cst


def kernel(x_prompt, x_sample, state_hgrn, state_conv, meta_tokens, norm_mix_g, w_in, lb_logits,
           hgrn_norm_g, conv_w, conv_b, gn_g, gn_b, w_out, norm_ffn_g, w_ffn_gate, w_ffn_up,
           w_ffn_down, norm_final_g):
    f32 = np.float32
    x_prompt = np.asarray(x_prompt, f32)
    x_sample = np.asarray(x_sample, f32)
    state_hgrn = np.asarray(state_hgrn, f32)
    state_conv = np.asarray(state_conv, f32)
    meta = np.asarray(meta_tokens, f32)
    if "nc" not in _CACHE:
        _CACHE["nc"] = build_nc()
    nc = _CACHE["nc"]

    def chunks(v):
        v = np.asarray(v, f32).reshape(-1, 128)
        return v.T

    vecT = np.zeros((128, NV), f32)
    lbl = np.asarray(lb_logits, f32)
    vecT[:, V_LB0:V_LB0 + 8] = chunks(lbl[0])
    vecT[:, V_LB1:V_LB1 + 8] = chunks(lbl[1])
    vecT[:, V_HG:V_HG + 8] = chunks(np.asarray(hgrn_norm_g, f32)[0])
    vecT[:, V_CB:V_CB + 8] = chunks(np.asarray(conv_b, f32)[0])
    vecT[:, V_GNG:V_GNG + 8] = chunks(np.asarray(gn_g, f32)[0])
    vecT[:, V_GNB:V_GNB + 8] = chunks(np.asarray(gn_b, f32)[0])
    vecT[:, V_GMIX:V_GMIX + 16] = chunks(np.asarray(norm_mix_g, f32)[0])
    vecT[:, V_GFFN:V_GFFN + 16] = chunks(np.asarray(norm_ffn_g, f32)[0])
    cw = np.asarray(conv_w, f32)[0]
    vecT[:, V_CW:V_CW + 248] = cw.reshape(31, 8, 128).transpose(2, 1, 0).reshape(128, 248)
    vecT[:, V_W30:V_W30 + 8] = chunks(cw[30])
    cst = _consts()
    gfin = np.ascontiguousarray(np.broadcast_to(np.asarray(norm_final_g, f32)[None, :], (128, D)))
    wrep = np.ascontiguousarray(np.tile(cw[0:30], (4, 1)))
    w_in0 = np.ascontiguousarray(np.asarray(w_in, f32)[0])
    w_out0 = np.ascontiguousarray(np.asarray(w_out, f32)[0])
    w_g0 = np.ascontiguousarray(np.asarray(w_ffn_gate, f32)[0])
    w_u0 = np.ascontiguousarray(np.asarray(w_ffn_up, f32)[0])
    w_d0 = np.ascontiguousarray(np.asarray(w_ffn_down, f32)[0])

    in_maps = []
    for c in range(8):
        b, j = c // 2, c % 2
        hp = np.concatenate([meta, x_prompt[b]], axis=0)
        if j == 0:
            xpre = np.concatenate([np.zeros((1024, D), f32), meta], axis=0)
            xmain = hp[16:1040]
        else:
            xpre = hp[0:1040]
            xmain = hp[1040:2064]
        xs = x_sample[16 * c:16 * c + 16, 0, :]
        xall = np.ascontiguousarray(np.concatenate([xpre, xmain, xs], axis=0))
        in_maps.append({
            "xall": xall,
            "sh": np.ascontiguousarray(state_hgrn[0, 16 * c:16 * c + 16]),
            "sc": np.ascontiguousarray(state_conv[0, 16 * c:16 * c + 16]),
            "vecT": vecT, "cst": cst, "gfin": gfin, "wrep": wrep,
            "w_in": w_in0, "w_out": w_out0, "w_gate": w_g0, "w_up": w_u0, "w_down": w_d0,
        })
    res = run_bass_kernel_spmd(nc, in_maps, core_ids=list(range(8)))
    R = res.results
    y_prompt = np.zeros((4, 2048, D), f32)
    y_sample = np.zeros((128, 1, D), f32)
    nshp = np.zeros((1, 4, 8, 128, 128), f32)
    nscp = np.zeros((1, 4, 30, 1024), f32)
    nshs = np.zeros((1, 128, 8, 128, 128), f32)
    nscs = np.zeros((1, 128, 30, 1024), f32)
    for c in range(8):
        b, j = c // 2, c % 2
        r = R[c]
        y_prompt[b, j * 1024:(j + 1) * 1024] = r["y"][0:1024]
        y_sample[16 * c:16 * c + 16, 0] = r["y"][1024:1040]
        nshs[0, 16 * c:16 * c + 16] = r["shn"]
        nscs[0, 16 * c:16 * c + 16] = r["scn"]
        if j == 1:
            nshp[0, b] = r["sout"]
            nscp[0, b] = r["ctail"]
    return (y_prompt, y_sample, nshp, nscp, nshs, nscs)
```

```python
import numpy as np
from contextlib import ExitStack
import concourse.bass as bass
import concourse.mybir as mybir
from concourse.bass_utils import run_bass_kernel_spmd

F32 = mybir.dt.float32
BF = mybir.dt.bfloat16
AF = mybir.ActivationFunctionType
ALU = mybir.AluOpType

D = 2048
DFF = 5632
NFF = DFF // 128
EPS = 1e-6
NV = 336
V_W30 = 328
NCST = 960
NW = 53200

V_LB0, V_LB1, V_HG, V_CB, V_GNG, V_GNB, V_GMIX, V_GFFN, V_CW = 0, 8, 16, 24, 32, 40, 48, 64, 80
C_ID, C_MASK, C_RM, C_ONE, C_IND = 0, 128, 256, 768, 896


class Prog:
    ENG = ["pe", "act", "dve", "pool", "sp"]

    def __init__(self, nc, es):
        self.nc = nc
        self.es = es
        self.ops = {e: [] for e in self.ENG}
        self.sem = {e: es.enter_context(nc.semaphore("s_" + e)) for e in self.ENG}
        self.waited = {e: {} for e in self.ENG}
        self.res_w = {}
        self.res_r = {}
        self.dsem = {}
        self.last_tokens = {}
        self.needed = {e: set() for e in self.ENG}

    @staticmethod
    def _tkey(tok):
        return tok[1] if tok[0] == "op" else ("dma", tok[1])

    def _filter(self, eng, deps, skip_self=False):
        best = {}
        for tok in deps:
            k = self._tkey(tok)
            if k == eng and (eng == "pe" or skip_self):
                continue
            if k not in best or best[k][2] < tok[2]:
                best[k] = tok
        waits = []
        for k, tok in best.items():
            if self.waited[eng].get(k, -1) >= tok[2]:
                continue
            self.waited[eng][k] = tok[2]
            waits.append(tok)
            if tok[0] == "op":
                self.needed[tok[1]].add(tok[2])
        return waits

    def _deps(self, eng, reads, writes, skip_self=False):
        deps = []
        for r in reads:
            if r in self.res_w:
                deps.append(self.res_w[r])
        for w in writes:
            if w in self.res_w:
                deps.append(self.res_w[w])
            deps.extend(self.res_r.get(w, []))
        return self._filter(eng, deps, skip_self)

    def _commit(self, tok, reads, writes):
        for r in reads:
            self.res_r.setdefault(r, []).append(tok)
        for w in writes:
            self.res_w[w] = tok
            self.res_r[w] = []
        self.last_tokens[self._tkey(tok)] = tok

    def add(self, eng, fn, reads=(), writes=(), skip_self=False):
        waits = self._deps(eng, reads, writes, skip_self)
        idx = len(self.ops[eng])
        tok = ("op", eng, idx)
        self.ops[eng].append([waits, fn, True])
        self._commit(tok, reads, writes)
        return tok

    def dma(self, eng, pairs, reads=(), writes=(), key=None):
        waits = self._deps(eng, reads, writes)
        if key not in self.dsem:
            self.dsem[key] = [self.es.enter_context(self.nc.semaphore("d_" + str(len(self.dsem)))), 0]
        ent = self.dsem[key]
        ent[1] += len(pairs)
        sem = ent[0]
        tok = ("dma", key, 16 * ent[1])

        def fn(e, pairs=pairs, sem=sem):
            for (o, i) in pairs:
                e.dma_start(out=o, in_=i).then_inc(sem, 16)
            return None

        self.ops[eng].append([waits, fn, False])
        self._commit(tok, reads, writes)
        return tok

    def barrier(self):
        toks = list(self.last_tokens.values())
        for e in self.ENG:
            waits = self._filter(e, [t for t in toks if self._tkey(t) != e])
            if waits:
                self.ops[e].append([waits, None, False])
        self.res_w = {}
        self.res_r = {}

    def wait_all_dma(self, eng):
        toks = [("dma", key, 16 * cnt) for key, (sem, cnt) in self.dsem.items()]
        self.ops[eng].append([toks, None, False])

    def emit(self, block):
        count = {}
        for e in self.ENG:
            c = 0
            m = {}
            for idx in range(len(self.ops[e])):
                if idx in self.needed[e]:
                    c += 1
                    m[idx] = c
            count[e] = m

        def run(eng_name):
            def body(e):
                for idx, (waits, fn, sigable) in enumerate(self.ops[eng_name]):
                    for tok in waits:
                        if tok[0] == "op":
                            e.wait_ge(self.sem[tok[1]], count[tok[1]][tok[2]])
                        else:
                            e.wait_ge(self.dsem[tok[1]][0], tok[2])
                    if fn is None:
                        continue
                    ins = fn(e)
                    if sigable and idx in count[eng_name]:
                        ins.then_inc(self.sem[eng_name], 1)
            return body

        block.tensor(run("pe"))
        block.scalar(run("act"))
        block.vector(run("dve"))
        block.gpsimd(run("pool"))
        block.sync(run("sp"))


def build_nc():
    nc = bass.Bass("TRN2", target_bir_lowering=False)
    dt = nc.dram_tensor
    xall = dt("xall", [2080, D], F32, kind="ExternalInput").ap()
    sh = dt("sh", [16, 8, 128, 128], F32, kind="ExternalInput").ap()
    sc = dt("sc", [16, 30, 1024], F32, kind="ExternalInput").ap()
    vec_d = dt("vecT", [128, NV], F32, kind="ExternalInput").ap()
    cst_d = dt("cst", [128, NCST], F32, kind="ExternalInput").ap()
    gfin_d = dt("gfin", [128, D], F32, kind="ExternalInput").ap()
    wrep_d = dt("wrep", [120, 1024], F32, kind="ExternalInput").ap()
    w_in = dt("w_in", [D, 6144], F32, kind="ExternalInput").ap()
    w_out = dt("w_out", [D, D], F32, kind="ExternalInput").ap()
    w_gate = dt("w_gate", [D, DFF], F32, kind="ExternalInput").ap()
    w_up = dt("w_up", [D, DFF], F32, kind="ExternalInput").ap()
    w_down = dt("w_down", [DFF, D], F32, kind="ExternalInput").ap()
    y_d = dt("y", [1040, D], F32, kind="ExternalOutput").ap()
    sout_d = dt("sout", [8, 128, 128], F32, kind="ExternalOutput").ap()
    ctail_d = dt("ctail", [30, 1024], F32, kind="ExternalOutput").ap()
    shn_d = dt("shn", [16, 8, 128, 128], F32, kind="ExternalOutput").ap()
    scn_d = dt("scn", [16, 30, 1024], F32, kind="ExternalOutput").ap()
    hscr = dt("hscr", [1040, D], F32).ap()
    h2scr = dt("h2scr", [1040, D], F32).ap()

    w_in_v = w_in.rearrange("(k p) n -> p k n", p=128)
    w_out_v = w_out.rearrange("(k p) n -> p k n", p=128)
    w_gate_v = w_gate.rearrange("(k p) n -> p k n", p=128)
    w_up_v = w_up.rearrange("(k p) n -> p k n", p=128)
    w_down_v = w_down.rearrange("(k p) n -> p k n", p=128)

    with ExitStack() as es:
        arena = es.enter_context(nc.sbuf_tensor("arena", [128, NW], F32))
        PA = es.enter_context(nc.psum_tensor("PA", [128, 1024], F32))
        PB = es.enter_context(nc.psum_tensor("PB", [128, 1024], F32))
        PC = es.enter_context(nc.psum_tensor("PC", [128, 1024], F32))
        PT = es.enter_context(nc.psum_tensor("PT", [128, 1024], F32))
        PTb = PT[:, :].bitcast(BF)
        P = Prog(nc, es)

        ptr = [0]

        def alloc(nwords, dtype=F32):
            o = ptr[0]
            ptr[0] += nwords
            assert ptr[0] <= NW, ("arena overflow", ptr[0])
            v = arena[:, o:o + nwords]
            if dtype == BF:
                v = v.bitcast(BF)
            return v

        def mark():
            return ptr[0]

        def reset(m):
            ptr[0] = m

        vecT = alloc(NV)
        cst = alloc(NCST)
        identb = alloc(64, BF)
        onesb = alloc(64, BF)
        indb = alloc(32, BF)
        oml = alloc(8)
        lbv = alloc(8)
        Sst = alloc(1024)
        Sbf = alloc(512, BF)
        ss1 = alloc(4)
        junk1 = alloc(8)
        Sst3 = Sst.rearrange("p (h v) -> p h v", v=128)
        Sbf3 = Sbf.rearrange("p (h v) -> p h v", v=128)
        indb3 = indb.rearrange("p (t s) -> p t s", s=16)
        ident_f = cst[:, C_ID:C_ID + 128]
        mask2 = cst[:, C_MASK:C_MASK + 128]
        rmask = cst[:, C_RM:C_RM + 512]

        P.dma("sp", [(vecT, vec_d), (cst, cst_d)], writes=["vecT", "cst"], key="cst")
        P.add("dve", lambda e: e.tensor_copy(out=identb, in_=ident_f), ["cst"], ["identb"])
        P.add("dve", lambda e: e.tensor_copy(out=onesb, in_=cst[:, C_ONE:C_ONE + 128]), ["cst"], ["onesb"])
        P.add("dve", lambda e: e.tensor_copy(out=indb, in_=cst[:, C_IND:C_IND + 64]), ["cst"], ["indb"])
        P.add("dve", lambda e: e.tensor_tensor(out=lbv, in0=vecT[:, V_LB0:V_LB0 + 8], in1=vecT[:, V_LB1:V_LB1 + 8],
                                               op=ALU.subtract), ["vecT"], ["lbv"])
        P.add("act", lambda e: e.activation(out=lbv, in_=lbv, func=AF.Sigmoid), ["lbv"], ["lbv"])
        P.add("act", lambda e: e.activation(out=oml, in_=lbv, func=AF.Identity, scale=-1.0, bias=1.0),
              ["lbv"], ["oml"])
        P.add("dve", lambda e: e.memset(Sst, 0.0), [], ["S"])
        P.add("dve", lambda e: e.memset(Sbf, 0.0), [], ["Sbf"])

        base_mark = mark()

        mixT = alloc(8320, BF).rearrange("p (c t) -> p c t", t=1040)
        R_A1 = alloc(8320)
        A1 = R_A1.bitcast(BF)
        scratch_mark = mark()
        NT = 528
        xT = A1[:, 0:16 * NT].rearrange("p (k t) -> p k t", t=NT)
        wbufs = [alloc(2048, BF).rearrange("p (k n) -> p k n", n=256) for _ in range(4)]
        R_Qt = alloc(2112)
        R_Kt = alloc(2112)
        R_KdT = alloc(2112)
        Qt = R_Qt.bitcast(BF).rearrange("p (h t) -> p h t", t=NT)
        Qb = alloc(2112, BF).rearrange("p (h t) -> p h t", t=NT)
        Kt = R_Kt.bitcast(BF).rearrange("p (h t) -> p h t", t=NT)
        KdT = R_KdT.bitcast(BF).rearrange("p (h t) -> p h t", t=NT)
        R_Kd = alloc(2560)
        R_Vt = alloc(2560)
        Kd = R_Kd.bitcast(BF).rearrange("p (i c) -> p i c", c=1024)
        Vt = R_Vt.bitcast(BF).rearrange("p (i c) -> p i c", c=1024)
        ubuf = alloc(2232, BF).rearrange("p (g t) -> p g t", t=558)
        R_D0 = alloc(2232)
        Dg = [R_D0[:, 0:1984].bitcast(BF).rearrange("p (j c) -> p j c", c=128),
              R_KdT[:, 0:1984].bitcast(BF).rearrange("p (j c) -> p j c", c=128)]
        DK = [["D0"], ["KdT%d" % h for h in range(8)]]
        JUNK_O = mark()
        tmpA = [alloc(512) for _ in range(6)]
        xin = [arena[:, JUNK_O:JUNK_O + 2048], R_A1[:, 4224:4224 + 2048]]
        XK = [["t0a", "t1a", "t2a", "t3a"], ["t0b", "t1b", "t2b", "t3b"]]
        xnb_m = arena[:, JUNK_O + 2048:JUNK_O + 3072].bitcast(BF)
        junk_m = R_A1[:, 4224 + 2048:4224 + 3072].bitcast(BF)
        tmpB = [R_A1[:, 4224 + i * 512:4224 + (i + 1) * 512] for i in range(6)]
        tset_i = [0]

        def next_tset():
            i = tset_i[0] % 2
            tset_i[0] += 1
            return ((tmpA, "a") if i == 0 else (tmpB, "b"))
        ebuf = alloc(8 * 9).rearrange("p (h c) -> p h c", c=9)
        R_scm = alloc(512)
        scm = R_scm.bitcast(BF).rearrange("p (h t) -> p h t", t=128)
        sgt = R_scm[:, 0:256]
        osq = alloc(512, BF)
        big = [alloc(1024) for _ in range(2)]
        utok = big[0]
        utok2 = big[1]
        xT_tail = alloc(240, BF).rearrange("p (k t) -> p k t", t=30)
        fgS = alloc(128).rearrange("p (h s) -> p h s", s=16)
        kgS = alloc(128).rearrange("p (h s) -> p h s", s=16)
        scin1 = R_Kd[:, 0:1024]
        wrep = R_Kd[:, 1024:2048]
        prodb = [R_Vt[:, tt * 512:(tt + 1) * 512].bitcast(BF) for tt in range(4)]
        onehot3 = R_KdT[:, 0:1024].bitcast(BF).rearrange("p (s m) -> p s m", m=128)
        S0b = [R_Qt[:, 0:1024], R_Qt[:, 1024:2048], R_KdT[:, 1024:2048], R_Kt[:, 512:1536]]
        Snbs = [R_Kt[:, 0:512].bitcast(BF), R_Kt[:, 1536:2048].bitcast(BF)]

        accs = [(PA[:, 0:512], "PA0"), (PA[:, 512:1024], "PA1"), (PB[:, 0:512], "PB0"),
                (PB[:, 512:1024], "PB1"), (PC[:, 0:512], "PC0"), (PC[:, 512:1024], "PC1")]
        acc_i = [0]

        def next_acc():
            a = accs[acc_i[0] % 6]
            acc_i[0] += 1
            return a

        wb_i = [0]

        def stream_w(view, c0, ncols=256, K=16):
            i = wb_i[0] % 4
            wb_i[0] += 1
            buf = wbufs[i]
            P.dma("pool", [(buf[:, 0:K, 0:ncols], view[:, 0:K, c0:c0 + ncols])],
                  writes=["wb%d" % i], key="wb%d" % i)
            return buf, "wb%d" % i

        PT3 = PTb.rearrange("p (k t) -> p k t", t=128)
        xin_i = [0]

        def prep_tile(src, n, dstT, t0, gcol, dst_key="xT"):
            i = xin_i[0] % 2
            xin_i[0] += 1
            xt = xin[i]
            kx = XK[i]
            P.dma("sp", [(xt[0:n, :], src)], writes=list(kx), key="ld_xin%d" % i)
            norm_transpose(xt, kx, n, dstT, t0, gcol, dst_key, xnb_m, ["t4a", "t5a"], junk_m, ["t4b", "t5b"])

        prep_scale_on_act = [False]

        def norm_transpose(xt, kx, n, dstT, t0, gcol, dst_key, xnb, xk, junk=None, jk=None):
            kx = list(kx) if isinstance(kx, (list, tuple)) else [kx]
            xk = list(xk) if isinstance(xk, (list, tuple)) else [xk]
            if junk is None:
                junk, jk = xnb, xk
            P.add("act", lambda e: e.activation(out=junk[0:n, :], in_=xt[0:n, :], func=AF.Square,
                                                accum_out=ss1[0:n, 0:1]), kx, list(jk) + ["ss1"])
            P.add("act", lambda e: e.activation(out=ss1[0:n, 1:2], in_=ss1[0:n, 0:1], func=AF.Sqrt,
                                                scale=1.0 / D, bias=EPS), ["ss1"], ["ss1b"])
            P.add("dve", lambda e: e.reciprocal(out=ss1[0:n, 2:3], in_=ss1[0:n, 1:2]), ["ss1b"], ["ss1c"])
            if prep_scale_on_act[0]:
                P.add("act", lambda e: e.activation(out=xnb[0:n, :], in_=xt[0:n, :], func=AF.Copy,
                                                    scale=ss1[0:n, 2:3]), kx + ["ss1c"], xk)
            else:
                P.add("dve", lambda e: e.tensor_scalar(out=xnb[0:n, :], in0=xt[0:n, :], scalar1=ss1[0:n, 2:3],
                                                       scalar2=None, op0=ALU.mult), kx + ["ss1c"], xk)

            def tr(e):
                ins = None
                for k in range(16):
                    ins = e.transpose(out=PT3[:, k, 0:n], in_=xnb[0:n, k * 128:(k + 1) * 128],
                                      identity=identb[0:n, 0:n])
                return ins
            P.add("pe", tr, xk + ["identb"], ["PT"])
            P.add("dve", lambda e: e.tensor_tensor(
                out=dstT[:, :, t0:t0 + n], in0=PT3[:, :, 0:n],
                in1=vecT[:, gcol:gcol + 16].unsqueeze(2).to_broadcast([128, 16, n]), op=ALU.mult),
                ["PT", "vecT"], [dst_key])

        def mm_feat(wbuf, wkey, m, rhsT, t0, n, rkey, K=16):
            acc, akey = next_acc()

            def fn(e):
                ins = None
                for k in range(K):
                    ins = e.matmul(acc[:, 0:n], wbuf[:, k, m * 128:(m + 1) * 128], rhsT[:, k, t0:t0 + n],
                                   start=(k == 0), stop=(k == K - 1))
                return ins
            P.add("pe", fn, [wkey, rkey], [akey])
            return acc, akey

        def lockstep(gens):
            gens = [g_ for g_ in gens if g_ is not None]
            while gens:
                for g_ in list(gens):
                    try:
                        next(g_)
                    except StopIteration:
                        gens.remove(g_)

        def drain(gen):
            if gen is not None:
                for _ in gen:
                    pass

        def mm_into(ps_ap, pkeys, wbuf, wkey, m, rhsT, t0, n, rkey):
            def fn(e):
                ins = None
                for k in range(16):
                    ins = e.matmul(ps_ap, wbuf[:, k, m * 128:(m + 1) * 128], rhsT[:, k, t0:t0 + n],
                                   start=(k == 0), stop=(k == 15))
                return ins
            P.add("pe", fn, [wkey, rkey], pkeys)

        PTs = PT[:, :].rearrange("p (a g s) -> p a g s", g=8, s=16)

        def gn_chain_s():
            tmp = tmpA
            cacc = tmp[5][:, 0:128]
            c3 = cacc.rearrange("p (g s) -> p g s", s=16)
            cb = tmp[1][:, 0:128].bitcast(BF)[:, 0:128]
            sqb = tmp[2][:, 0:128].bitcast(BF)[:, 0:128]
            P.add("act", lambda e: e.activation(out=cb, in_=cacc, func=AF.Copy), ["t5a"], ["t1a"])
            P.add("act", lambda e: e.activation(out=sqb, in_=cacc, func=AF.Square), ["t5a"], ["t2a"])
            (m_ps, mkey) = next_acc()
            (q_ps, qkey) = next_acc()
            P.add("pe", lambda e: e.matmul(m_ps[:, 0:128], onesb, cb, start=True, stop=True), ["t1a", "onesb"], [mkey])
            P.add("pe", lambda e: e.matmul(q_ps[:, 0:128], onesb, sqb, start=True, stop=True), ["t2a", "onesb"], [qkey])
            mean = tmp[3][:, 0:128]
            m2 = tmp[4][:, 0:128]
            mean3 = mean.rearrange("p (g s) -> p g s", s=16)
            P.add("act", lambda e: e.activation(out=mean, in_=m_ps[:, 0:128], func=AF.Copy, scale=1.0 / 128),
                  [mkey], ["t3a"])
            P.add("dve", lambda e: e.tensor_tensor(out=m2, in0=mean, in1=mean, op=ALU.mult), ["t3a"], ["t4a"])
            P.add("dve", lambda e: e.scalar_tensor_tensor(out=m2, in0=q_ps[:, 0:128], scalar=1.0 / 128, in1=m2,
                                                          op0=ALU.mult, op1=ALU.subtract), [qkey, "t4a"], ["t4a"])
            P.add("act", lambda e: e.activation(out=m2, in_=m2, func=AF.Ln, bias=EPS), ["t4a"], ["t4a"])
            P.add("act", lambda e: e.activation(out=m2, in_=m2, func=AF.Exp, scale=-0.5), ["t4a"], ["t4a"])
            P.add("dve", lambda e: e.tensor_tensor(out=mean, in0=cacc, in1=mean, op=ALU.subtract),
                  ["t5a", "t3a"], ["t3a"])
            P.add("dve", lambda e: e.tensor_tensor(out=mean, in0=mean, in1=m2, op=ALU.mult), ["t3a", "t4a"], ["t3a"])
            P.add("dve", lambda e: e.tensor_tensor(
                out=mean3, in0=mean3, in1=vecT[:, V_GNG:V_GNG + 8].unsqueeze(2).to_broadcast([128, 8, 16]),
                op=ALU.mult), ["t3a", "vecT"], ["t3a"])
            P.add("dve", lambda e: e.tensor_tensor(
                out=mean3, in0=mean3, in1=vecT[:, V_GNB:V_GNB + 8].unsqueeze(2).to_broadcast([128, 8, 16]),
                op=ALU.add), ["t3a", "vecT"], ["t3a"])
            sgm = tmp[2][:, 0:128]
            P.add("act", lambda e: e.activation(out=sgm, in_=mean, func=AF.Sigmoid), ["t3a"], ["t2a"])
            P.add("dve", lambda e: e.tensor_tensor(
                out=mixT[:, 8:16, 1024:1040], in0=mean3, in1=sgm.rearrange("p (g s) -> p g s", s=16),
                op=ALU.mult), ["t3a", "t2a"], ["mixT"])

        def gn_chain(cacc, ckey, g, n, col0, tmp, sx):
            cb = tmp[1][:, 0:n].bitcast(BF)[:, 0:n]
            sqb = tmp[2][:, 0:n].bitcast(BF)[:, 0:n]
            P.add("act", lambda e: e.activation(out=cb, in_=cacc, func=AF.Copy), [ckey], ["t1" + sx])
            yield
            P.add("act", lambda e: e.activation(out=sqb, in_=cacc, func=AF.Square), [ckey], ["t2" + sx])
            yield
            (m_ps, mkey) = next_acc()
            (q_ps, qkey) = next_acc()
            P.add("pe", lambda e: e.matmul(m_ps[:, 0:n], onesb, cb, start=True, stop=True), ["t1" + sx, "onesb"], [mkey])
            yield
            P.add("pe", lambda e: e.matmul(q_ps[:, 0:n], onesb, sqb, start=True, stop=True), ["t2" + sx, "onesb"], [qkey])
            yield
            mean = tmp[3][:, 0:n]
            P.add("act", lambda e: e.activation(out=mean, in_=m_ps[:, 0:n], func=AF.Copy, scale=1.0 / 128),
                  [mkey], ["t3" + sx])
            yield
            m2 = tmp[4][:, 0:n]
            P.add("dve", lambda e: e.tensor_tensor(out=m2, in0=mean, in1=mean, op=ALU.mult), ["t3" + sx], ["t4" + sx])
            yield
            P.add("dve", lambda e: e.scalar_tensor_tensor(out=m2, in0=q_ps[:, 0:n], scalar=1.0 / 128, in1=m2,
                                                          op0=ALU.mult, op1=ALU.subtract), [qkey, "t4" + sx], ["t4" + sx])
            yield
            P.add("act", lambda e: e.activation(out=m2, in_=m2, func=AF.Ln, bias=EPS), ["t4" + sx], ["t4" + sx])
            yield
            P.add("act", lambda e: e.activation(out=m2, in_=m2, func=AF.Exp, scale=-0.5), ["t4" + sx], ["t4" + sx])
            yield
            P.add("dve", lambda e: e.tensor_tensor(out=mean, in0=cacc, in1=mean, op=ALU.subtract),
                  [ckey, "t3" + sx], ["t3" + sx])
            yield
            P.add("dve", lambda e: e.tensor_tensor(out=mean, in0=mean, in1=m2, op=ALU.mult), ["t3" + sx, "t4" + sx], ["t3" + sx])
            yield
            P.add("dve", lambda e: e.tensor_scalar(out=mean, in0=mean, scalar1=vecT[:, V_GNG + g:V_GNG + g + 1],
                                                   scalar2=vecT[:, V_GNB + g:V_GNB + g + 1], op0=ALU.mult,
                                                   op1=ALU.add), ["t3" + sx, "vecT"], ["t3" + sx])
            yield
            sgm = tmp[2][:, 0:n]
            P.add("act", lambda e: e.activation(out=sgm, in_=mean, func=AF.Sigmoid), ["t3" + sx], ["t2" + sx])
            yield
            P.add("dve", lambda e: e.tensor_tensor(out=mixT[:, 8 + g, col0:col0 + n], in0=mean, in1=sgm,
                                                   op=ALU.mult), ["t3" + sx, "t2" + sx], ["mixT"])
            yield

        def onorm_a(n):
            PB3 = PB[:, :].rearrange("p (h t) -> p h t", t=128)
            osq3 = osq.rearrange("p (h t) -> p h t", t=128)
            o3 = big[1].rearrange("p (h t) -> p h t", t=128)

            def ocp(e):
                ins = None
                for h in range(8):
                    ins = e.activation(out=o3[:, h, 0:n], in_=PB3[:, h, 0:n], func=AF.Copy,
                                       scale=vecT[:, V_HG + h:V_HG + h + 1])
                return ins
            P.add("act", ocp, ["PB0", "PB1", "vecT"], ["big1"])
            P.add("act", lambda e: e.activation(out=osq3[:, :, 0:n], in_=PB3[:, :, 0:n], func=AF.Square),
                  ["PB0", "PB1"], ["osq"])

        def onorm_b(n, col0):
            PT4 = PA[:, :].rearrange("p (h t) -> p h t", t=128)
            osq3 = osq.rearrange("p (h t) -> p h t", t=128)
            r3 = big[0].rearrange("p (h t) -> p h t", t=128)
            o3 = big[1].rearrange("p (h t) -> p h t", t=128)

            def fn(e):
                ins = None
                for h in range(8):
                    ins = e.matmul(PT4[:, h, 0:n], onesb, osq3[:, h, 0:n], start=True, stop=True)
                return ins
            P.add("pe", fn, ["osq", "onesb"], ["PA0", "PA1"])
            P.add("act", lambda e: e.activation(out=r3[:, :, 0:n], in_=PT4[:, :, 0:n], func=AF.Ln,
                                                scale=1.0 / 128, bias=EPS), ["PA0", "PA1"], ["big0"])
            P.add("act", lambda e: e.activation(out=r3[:, :, 0:n], in_=r3[:, :, 0:n], func=AF.Exp, scale=-0.5),
                  ["big0"], ["big0"])
            P.add("dve", lambda e: e.tensor_tensor(out=o3[:, :, 0:n], in0=o3[:, :, 0:n], in1=r3[:, :, 0:n],
                                                   op=ALU.mult), ["big1", "big0"], ["big1"])
            P.add("dve", lambda e: e.tensor_tensor(out=mixT[:, 0:8, col0:col0 + n], in0=o3[:, :, 0:n],
                                                   in1=mixT[:, 0:8, col0:col0 + n], op=ALU.mult),
                  ["big1", "mixT"], ["mixT"])

        def onorm_chain(n, col0):
            onorm_a(n)
            onorm_b(n, col0)

        PSR = {"PA": (PA, ["PA0", "PA1"]), "PB": (PB, ["PB0", "PB1"]), "PC": (PC, ["PC0", "PC1"]),
               "PT": (PT, ["PT"])}

        def ds_matmul(tile_i, r0, rn, reg):
            ps3 = PSR[reg][0][:, :].rearrange("p (h v) -> p h v", v=128)

            def fn(e):
                ins = None
                for h in range(8):
                    ins = e.matmul(ps3[:, h, :], Kd[r0:r0 + rn, tile_i, h * 128:(h + 1) * 128],
                                   Vt[r0:r0 + rn, tile_i, h * 128:(h + 1) * 128], start=True, stop=True)
                return ins
            P.add("pe", fn, ["RK", "RV"], PSR[reg][1])

        def s_apply(chunk_idx, reg, want_bf):
            ps3 = PSR[reg][0][:, :].rearrange("p (h v) -> p h v", v=128)
            def supd(e):
                ins = None
                for h in range(8):
                    ins = e.scalar_tensor_tensor(out=Sst3[:, h, :], in0=Sst3[:, h, :],
                                                 scalar=ebuf[:, h, chunk_idx:chunk_idx + 1], in1=ps3[:, h, :],
                                                 op0=ALU.mult, op1=ALU.add)
                return ins
            P.add("dve", supd, ["S", "ebuf"] + PSR[reg][1], ["S"])
            if want_bf:
                P.add("dve", lambda e: e.tensor_copy(out=Sbf, in_=Sst), ["S"], ["Sbf"])

        def mixer_pass(row0, blocks, is_main, has_samples, tail_gagb, mix_col0, hook=None, save_tail=False,
                       use_tail=False):
            nprompt = sum(n for _, n in blocks)
            ntot = nprompt + (16 if has_samples else 0)
            tiles = []
            t = 0
            while t < ntot:
                n = min(128, ntot - t)
                tiles.append((t, n))
                t += n
            for (t0, n) in tiles:
                prep_tile(xall[row0 + t0:row0 + t0 + n, :], n, xT, t0, V_GMIX)
            mblocks = list(blocks) + ([(512, 16)] if has_samples else [])
            if save_tail:
                P.add("act", lambda e: e.activation(out=xT_tail, in_=xT[:, :, nprompt - 30:nprompt], func=AF.Copy),
                      ["xT"], ["xTtail"])
            if hook is not None:
                hook()

            if is_main or tail_gagb:
                cblocks = list(blocks) if is_main else [(nprompt - 30, 30)]
                chains = []
                built = set()

                def conv_front(wa, ka, wg, kg_, m, g, t0, n, ch, tail=False):
                    is_s = has_samples and t0 == 512
                    src, skey = (xT_tail, "xTtail") if tail else (xT, "xT")
                    a_ps, akey = mm_feat(wa, ka, m, src, t0, n, skey)
                    b_ps, bkey = mm_feat(wg, kg_, m, src, t0, n, skey)
                    if tail:
                        sg, sgk = sgt[:, 0:n], "scm"
                    else:
                        tmp, sx = next_tset()
                        ch["tmp"], ch["sx"] = tmp, sx
                        sg, sgk = tmp[0][:, 0:n], "t0" + sx
                    P.add("act", lambda e: e.activation(out=sg, in_=b_ps[:, 0:n], func=AF.Sigmoid),
                          [bkey], [sgk])
                    yield
                    if tail:
                        ucol = 0
                    elif is_main:
                        ucol = 30 + t0
                    else:
                        ucol = 30 + nprompt - 30
                    P.add("dve", lambda e: e.tensor_tensor(
                        out=ubuf[:, g, ucol:ucol + n], in0=a_ps[:, 0:n], in1=sg, op=ALU.mult),
                        [akey, sgk], ["u%d" % g])
                    yield
                    if is_main and not is_s and not tail and g not in built:
                        built.add(g)
                        P.add("dve", lambda e: e.tensor_tensor(
                            out=Dg[g % 2], in0=identb.unsqueeze(1).to_broadcast([128, 31, 128]),
                            in1=vecT[:, V_CW + g * 31:V_CW + g * 31 + 31].unsqueeze(2).to_broadcast([128, 31, 128]),
                            op=ALU.mult), ["identb", "vecT"], DK[g % 2])
                        yield

                def conv_mid(m, g, t0, n, ch):
                    if not is_main:
                        return
                    is_s = has_samples and t0 == 512
                    tmp, sx = ch["tmp"], ch["sx"]
                    cacc = tmp[5][:, 0:n]
                    ch["cacc"] = cacc
                    if not is_s:
                        (cps, ckey) = next_acc()

                        def cvf(e):
                            ins = None
                            for j in range(31):
                                ins = e.matmul(cps[:, 0:n], Dg[g % 2][:, j, :], ubuf[:, g, t0 + j:t0 + j + n],
                                               start=(j == 0), stop=(j == 30))
                            return ins
                        P.add("pe", cvf, DK[g % 2] + ["u%d" % g], [ckey])
                        yield
                        P.add("act", lambda e: e.activation(out=cacc, in_=cps[:, 0:n], func=AF.Identity,
                                                            bias=vecT[:, V_CB + g:V_CB + g + 1]),
                              [ckey, "vecT"], ["t5" + sx])
                        yield
                    else:
                        (cs_ps, cskey) = next_acc()

                        def csf(e):
                            ins = None
                            for tt in range(4):
                                ins = e.matmul(cs_ps[:, 0:16], prodb[tt][0:120, g * 128:(g + 1) * 128],
                                               indb3[0:120, tt, :], start=(tt == 0), stop=(tt == 3))
                            return ins
                        P.add("pe", csf, ["RV", "indb"], [cskey])
                        yield
                        P.add("dve", lambda e: e.scalar_tensor_tensor(
                            out=cacc, in0=ubuf[:, g, 542:558],
                            scalar=vecT[:, V_CW + g * 31 + 30:V_CW + g * 31 + 31],
                            in1=cs_ps[:, 0:16], op0=ALU.mult, op1=ALU.add),
                            ["u%d" % g, cskey, "vecT"], ["t5" + sx])
                        yield
                        P.add("dve", lambda e: e.tensor_scalar(
                            out=cacc, in0=cacc, scalar1=vecT[:, V_CB + g:V_CB + g + 1], scalar2=None,
                            op0=ALU.add), ["t5" + sx, "vecT"], ["t5" + sx])
                        yield

                def conv_back(m, g, t0, n, ch):
                    if not is_main:
                        return
                    is_s = has_samples and t0 == 512
                    col = (mix_col0 + t0) if not is_s else 1024
                    yield from gn_chain(ch["cacc"], "t5" + ch["sx"], g, n, col, ch["tmp"], ch["sx"])

                def pump(front_gen, final=False):
                    i = len(chains) - 1
                    if final:
                        i += 1
                    gens = [front_gen]
                    for (stage, k) in (("mid", i - 1), ("back", i - 2)):
                        if 0 <= k < len(chains) and stage not in chains[k]["done"]:
                            c_ = chains[k]
                            chains[k]["done"].add(stage)
                            gens.append((conv_mid if stage == "mid" else conv_back)(
                                c_["m"], c_["g"], c_["t0"], c_["n"], c_))
                    lockstep(gens)

                for gp in range(4):
                    wa, ka = stream_w(w_in_v, 4096 + gp * 256)
                    wg, kg_ = stream_w(w_in_v, 5120 + gp * 256)
                    for m in range(2):
                        g = gp * 2 + m
                        if use_tail:
                            drain(conv_front(wa, ka, wg, kg_, m, g, 0, 30, {}, tail=True))
                        for (t0, n) in cblocks:
                            ch = {"m": m, "g": g, "t0": t0, "n": n, "done": set()}
                            chains.append(ch)
                            pump(conv_front(wa, ka, wg, kg_, m, g, t0, n, ch))
                        if has_samples:
                            mm_into(PTs[:, 0, g, :], ["PT"], wa, ka, m, xT, 512, 16, "xT")
                            mm_into(PTs[:, 4, g, :], ["PT"], wg, kg_, m, xT, 512, 16, "xT")
                    if is_main and has_samples:
                        for (r0, rn, okey) in ((482, 30, "ctail"), (512, 16, "scn29")):
                            (ta, tka) = next_acc()
                            (tb, tkb) = next_acc()

                            def tma(e, ta=ta, r0=r0, rn=rn, wa=wa):
                                ins = None
                                for k in range(16):
                                    ins = e.matmul(ta[0:rn, 0:256], xT[:, k, r0:r0 + rn], wa[:, k, 0:256],
                                                   start=(k == 0), stop=(k == 15))
                                return ins

                            def tmb(e, tb=tb, r0=r0, rn=rn, wg=wg):
                                ins = None
                                for k in range(16):
                                    ins = e.matmul(tb[0:rn, 0:256], xT[:, k, r0:r0 + rn], wg[:, k, 0:256],
                                                   start=(k == 0), stop=(k == 15))
                                return ins
                            P.add("pe", tma, [ka, "xT"], [tka])
                            P.add("pe", tmb, [kg_, "xT"], [tkb])
                            P.add("act", lambda e, tb=tb, rn=rn: e.activation(
                                out=sgt[0:rn, :], in_=tb[0:rn, 0:256], func=AF.Sigmoid), [tkb], ["scm"])
                            urow = utok if okey == "ctail" else utok2
                            P.add("dve", lambda e, ta=ta, rn=rn, gp=gp, urow=urow: e.tensor_tensor(
                                out=urow[0:rn, gp * 256:(gp + 1) * 256], in0=ta[0:rn, 0:256], in1=sgt[0:rn, :],
                                op=ALU.mult), [tka, "scm"], ["big0" if okey == "ctail" else "big1"])
                pump(None, final=True)
                for k_ in range(len(chains)):
                    for stage in ("mid", "back"):
                        if stage not in chains[k_]["done"]:
                            chains[k_]["done"].add(stage)
                            c_ = chains[k_]
                            drain((conv_mid if stage == "mid" else conv_back)(c_["m"], c_["g"], c_["t0"], c_["n"], c_))
                if is_main and has_samples:
                    sgS_ = tmpA[0][:, 0:128]
                    P.add("act", lambda e: e.activation(out=sgS_, in_=PT[:, 512:640], func=AF.Sigmoid),
                          ["PT"], ["t0a"])
                    P.add("dve", lambda e: e.tensor_tensor(
                        out=ubuf[:, :, 542:558], in0=PTs[:, 0, :, :],
                        in1=sgS_.rearrange("p (g s) -> p g s", s=16), op=ALU.mult),
                        ["PT", "t0a"], ["u%d" % g for g in range(8)])

                    def csf_all(e):
                        ins = None
                        for g in range(8):
                            for tt in range(4):
                                ins = e.matmul(PTs[:, 1, g, :], prodb[tt][0:120, g * 128:(g + 1) * 128],
                                               indb3[0:120, tt, :], start=(tt == 0), stop=(tt == 3))
                        return ins
                    P.add("pe", csf_all, ["RV", "indb", "u0"], ["PT"])
                    cS3 = tmpA[5][:, 0:128].rearrange("p (g s) -> p g s", s=16)
                    P.add("dve", lambda e: e.tensor_tensor(
                        out=cS3, in0=ubuf[:, :, 542:558],
                        in1=vecT[:, V_W30:V_W30 + 8].unsqueeze(2).to_broadcast([128, 8, 16]), op=ALU.mult),
                        ["u%d" % g for g in range(8)] + ["vecT"], ["t5a"])
                    P.add("dve", lambda e: e.tensor_tensor(out=cS3, in0=cS3, in1=PTs[:, 1, :, :], op=ALU.add),
                          ["t5a", "PT"], ["t5a"])
                    P.add("dve", lambda e: e.tensor_tensor(
                        out=cS3, in0=cS3, in1=vecT[:, V_CB:V_CB + 8].unsqueeze(2).to_broadcast([128, 8, 16]),
                        op=ALU.add), ["t5a", "vecT"], ["t5a"])
                    gn_chain_s()
                    P.dma("sp", [(ctail_d, utok[0:30, :])], reads=["big0"], key="st_utok")
                    P.dma("sp", [(scn_d[:, 29, :], utok2[0:16, :])], reads=["big1"], key="st_utok2")

            if is_main:
                for (cbase, dst, dcol0) in ((3072, mixT, mix_col0), (0, Qb, 0)):
                    for gp in range(4):
                        w, wk = stream_w(w_in_v, cbase + gp * 256)
                        for m in range(2):
                            h = gp * 2 + m
                            for (t0, n) in mblocks:
                                ps, pk = mm_feat(w, wk, m, xT, t0, n, "xT")
                                if dst is mixT:
                                    c0_ = (1024 if (has_samples and t0 == 512) else mix_col0 + t0)
                                    dkey = "mixT"
                                else:
                                    c0_ = t0
                                    dkey = "Qb%d" % h
                                P.add("act", lambda e, ps=ps, dst=dst, h=h, c0_=c0_, n=n: e.activation(
                                    out=dst[:, h, c0_:c0_ + n], in_=ps[:, 0:n], func=AF.Silu), [pk], [dkey])

            chunk_of = {}
            nch = 0
            for (t0, n) in blocks:
                for c in range(0, n, 64):
                    chunk_of[t0 + c] = nch
                    nch += 1
            def f_chain(w, wk, m, h, t0, n, is_s):
                ps, pk = mm_feat(w, wk, m, xT, t0, n, "xT")
                tmp, sx = next_tset()
                fg = tmp[0][:, 0:n]
                lf = tmp[1][:, 0:n]
                kgt = tmp[2][:, 0:n]
                bb = tmp[3][:, 0:n]
                rr = tmp[4][:, 0:n]
                dd = tmp[5][:, 0:n]
                yield
                P.add("act", lambda e: e.activation(out=fg, in_=ps[:, 0:n], func=AF.Sigmoid), [pk], ["t0" + sx])
                yield
                P.add("dve", lambda e: e.tensor_scalar(
                    out=fg, in0=fg, scalar1=oml[:, h:h + 1], scalar2=lbv[:, h:h + 1], op0=ALU.mult,
                    op1=ALU.add), ["t0" + sx, "oml", "lbv"], ["t0" + sx])
                yield
                if is_s:
                    P.add("act", lambda e: e.activation(out=fgS[:, h, :], in_=fg, func=AF.Copy),
                          ["t0" + sx], ["fgS"])
                    yield
                    P.add("act", lambda e: e.activation(out=kgS[:, h, :], in_=fg, func=AF.Identity, scale=-1.0,
                                                        bias=1.0), ["t0" + sx], ["kgS"])
                    yield
                    return
                P.add("act", lambda e: e.activation(out=lf, in_=fg, func=AF.Ln), ["t0" + sx], ["t1" + sx])
                yield
                P.add("act", lambda e: e.activation(out=kgt, in_=fg, func=AF.Identity, scale=-1.0, bias=1.0),
                      ["t0" + sx], ["t2" + sx])
                P.add("dve", lambda e: e.tensor_tensor_scan(
                    out=bb, data0=rmask[:, 0:n], data1=lf, initial=0.0, op0=ALU.mult, op1=ALU.add),
                    ["t1" + sx, "cst"], ["t3" + sx])
                yield
                cw = min(64, n)
                ncb = n // cw
                bb3 = bb.rearrange("p (c t) -> p c t", t=cw)
                rr3 = rr.rearrange("p (c t) -> p c t", t=cw)
                dd3 = dd.rearrange("p (c t) -> p c t", t=cw)
                P.add("dve", lambda e: e.tensor_tensor(
                    out=rr3, in0=bb3[:, :, cw - 1:cw].to_broadcast([128, ncb, cw]), in1=bb3,
                    op=ALU.subtract), ["t3" + sx], ["t4" + sx])
                c_first = chunk_of[t0]
                P.add("act", lambda e: e.activation(
                    out=ebuf[:, h, c_first:c_first + ncb], in_=bb3[:, :, cw - 1], func=AF.Exp),
                    ["t3" + sx], ["ebuf"])
                yield
                if is_main:
                    mid = cw // 2 - 1
                    P.add("dve", lambda e: e.tensor_tensor(
                        out=dd3, in0=bb3, in1=bb3[:, :, mid:mid + 1].to_broadcast([128, ncb, cw]),
                        op=ALU.subtract), ["t3" + sx], ["t5" + sx])
                P.add("act", lambda e: e.activation(out=rr, in_=rr, func=AF.Exp), ["t4" + sx], ["t4" + sx])
                yield
                P.add("dve", lambda e: e.tensor_tensor(
                    out=KdT[:, h, t0:t0 + n], in0=kgt, in1=rr, op=ALU.mult), ["t4" + sx, "t2" + sx], ["KdT%d" % h])
                yield
                if is_main:
                    P.add("act", lambda e: e.activation(out=rr, in_=dd, func=AF.Exp), ["t5" + sx], ["t4" + sx])
                    P.add("act", lambda e: e.activation(out=dd, in_=dd, func=AF.Exp, scale=-1.0),
                          ["t5" + sx], ["t5" + sx])
                    P.add("act", lambda e: e.activation(out=bb, in_=bb, func=AF.Exp), ["t3" + sx], ["t3" + sx])
                    yield
                    P.add("dve", lambda e: e.tensor_tensor(
                        out=Qt[:, h, t0:t0 + n], in0=Qb[:, h, t0:t0 + n], in1=rr, op=ALU.mult),
                        ["t4" + sx, "Qb%d" % h], ["Qt%d" % h])
                    yield
                    P.add("dve", lambda e: e.tensor_tensor(
                        out=Kt[:, h, t0:t0 + n], in0=kgt, in1=dd, op=ALU.mult), ["t5" + sx, "t2" + sx], ["Kt%d" % h])
                    yield
                    P.add("dve", lambda e: e.tensor_tensor(
                        out=Qb[:, h, t0:t0 + n], in0=Qb[:, h, t0:t0 + n], in1=bb, op=ALU.mult),
                        ["t3" + sx, "Qb%d" % h, "Qt%d" % h], ["Qb%d" % h])
                    yield

            def i_gen(w, wk, cb):
                for ti, (t0, n) in enumerate(tiles):
                    (ps, pk) = next_acc()

                    def vfn(e, ps=ps, t0=t0, n=n):
                        ins = None
                        for k in range(16):
                            ins = e.matmul(ps[0:n, 0:256], xT[:, k, t0:t0 + n], w[:, k, 0:256],
                                           start=(k == 0), stop=(k == 15))
                        return ins
                    P.add("pe", vfn, [wk, "xT"], [pk])
                    yield
                    P.add("act", lambda e, ps=ps, ti=ti, n=n: e.activation(
                        out=Vt[0:n, ti, cb * 256:(cb + 1) * 256], in_=ps[0:n, 0:256], func=AF.Copy), [pk], ["RV"])
                    yield
                    yield

            for gp in range(4):
                w, wk = stream_w(w_in_v, 1024 + gp * 256)
                wi, wik = stream_w(w_in_v, 2048 + gp * 256)
                pblocks = [(t0, n) for (t0, n) in mblocks if not (has_samples and t0 == 512)]
                igen = i_gen(wi, wik, gp)
                for (t0, n) in pblocks:
                    lockstep([f_chain(w, wk, m, gp * 2 + m, t0, n, False) for m in range(2)] + [igen])
                    igen = None
                if has_samples:
                    for m in range(2):
                        mm_into(PTs[:, 0, gp * 2 + m, :], ["PT"], w, wk, m, xT, 512, 16, "xT")

            if has_samples:
                fg2 = fgS.rearrange("p h s -> p (h s)")
                P.add("act", lambda e: e.activation(out=fg2, in_=PT[:, 0:128], func=AF.Sigmoid), ["PT"], ["fgS"])
                P.add("dve", lambda e: e.tensor_tensor(
                    out=fgS, in0=fgS, in1=oml[:, 0:8].unsqueeze(2).to_broadcast([128, 8, 16]), op=ALU.mult),
                    ["fgS", "oml"], ["fgS"])
                P.add("dve", lambda e: e.tensor_tensor(
                    out=fgS, in0=fgS, in1=lbv[:, 0:8].unsqueeze(2).to_broadcast([128, 8, 16]), op=ALU.add),
                    ["fgS", "lbv"], ["fgS"])
                P.add("act", lambda e: e.activation(out=kgS.rearrange("p h s -> p (h s)"), in_=fg2,
                                                    func=AF.Identity, scale=-1.0, bias=1.0), ["fgS"], ["kgS"])

            ptiles = [(t0, n) for (t0, n) in tiles if t0 < nprompt]
            PTk = PTb[:, 0:1024].rearrange("p (h k) -> p h k", k=128)
            for ti, (t0, n) in enumerate(ptiles):
                def ktr(e, t0=t0, n=n):
                    ins = None
                    for h in range(8):
                        ins = e.transpose(out=PTk[0:n, h, :], in_=KdT[:, h, t0:t0 + n], identity=identb)
                    return ins
                P.add("pe", ktr, ["KdT%d" % h for h in range(8)] + ["identb"], ["PT"])
                P.add("dve", lambda e, ti=ti, n=n: e.tensor_copy(out=Kd[0:n, ti, :], in_=PTb[0:n, 0:1024]),
                      ["PT"], ["RK"])

            PA3 = PA[:, :].rearrange("p (h t) -> p h t", t=128)
            PB3 = PB[:, :].rearrange("p (h t) -> p h t", t=128)
            PT4 = PT[:, :].rearrange("p (h t) -> p h t", t=128)
            if not is_main:
                allch = []
                for ti, (t0, n) in enumerate(ptiles):
                    for c in range(0, n, 64):
                        allch.append((ti, c, min(64, n - c), chunk_of[t0 + c]))
                regs = ["PA", "PB", "PC"]
                for i, (ti, c, cn, cidx) in enumerate(allch):
                    ds_matmul(ti, c, cn, regs[i % 3])
                    s_apply(cidx, regs[i % 3], i == len(allch) - 1)
            else:
                pend_b = []
                for ti, (t0, n) in enumerate(ptiles):
                    chunks = [(c, min(64, n - c)) for c in range(0, n, 64)]

                    def scf(e, t0=t0, n=n):
                        ins = None
                        for h in range(8):
                            ins = e.matmul(PA3[0:n, h, 0:n], Kt[:, h, t0:t0 + n], Qt[:, h, t0:t0 + n],
                                           start=True, stop=True)
                        return ins
                    P.add("pe", scf, ["Kt%d" % h for h in range(8)] + ["Qt%d" % h for h in range(8)], ["PA0", "PA1"])
                    P.add("dve", lambda e, n=n: e.tensor_tensor(
                        out=scm[0:n, :, 0:n], in0=PA3[0:n, :, 0:n],
                        in1=mask2[0:n, 0:n].unsqueeze(1).to_broadcast([n, 8, n]), op=ALU.mult),
                        ["PA0", "PA1", "cst"], ["scm"])

                    def of1(e, ti=ti, n=n):
                        ins = None
                        for h in range(8):
                            ins = e.matmul(PB3[:, h, 0:n], Vt[0:n, ti, h * 128:(h + 1) * 128], scm[0:n, h, 0:n],
                                           start=(h % 4 == 0), stop=False, skip_group_check=True)
                        return ins
                    P.add("pe", of1, ["RV", "scm"], ["PB0", "PB1"])
                    dregs = ["PC", "PT"]
                    for ci, (c, cn) in enumerate(chunks):
                        ds_matmul(ti, c, cn, dregs[ci])
                    for ci, (c, cn) in enumerate(chunks):
                        def of2(e, t0=t0, c=c, cn=cn):
                            ins = None
                            for h in range(8):
                                ins = e.matmul(PB3[:, h, c:c + cn], Sbf3[:, h, :], Qb[:, h, t0 + c:t0 + c + cn],
                                               start=False, stop=True, skip_group_check=True)
                            return ins
                        P.add("pe", of2, ["Sbf"] + ["Qb%d" % h for h in range(8)], ["PB0", "PB1"])
                        s_apply(chunk_of[t0 + c], dregs[ci], True)
                    if pend_b:
                        pend_b.pop(0)()
                    onorm_a(n)
                    pend_b.append(lambda n=n, col=mix_col0 + t0: onorm_b(n, col))
                while pend_b:
                    pend_b.pop(0)()

            if is_main or tail_gagb:
                for g in range(8):
                    P.add("dve", lambda e, g=g: e.tensor_copy(out=ubuf[:, g, 0:30],
                                                              in_=ubuf[:, g, nprompt:nprompt + 30]),
                          ["u%d" % g], ["u%d" % g])

        P.add("dve", lambda e: e.memset(ubuf, 0.0), [], ["u%d" % g for g in range(8)])

        mixer_pass(0, [(0, 512)], False, False, False, 0)
        mixer_pass(512, [(0, 512), (512, 16)], False, False, False, 0, save_tail=True)
        mixer_pass(1040, [(0, 512)], True, False, False, 0, use_tail=True)

        P.dma("sp", [(scn_d[:, 0:29, :], sc[:, 1:30, :])], key="scn_copy", writes=["scn_rows"])

        def sample_hook():
            P.dma("sp", [(wrep[0:120, :], wrep_d)], writes=["RK"], key="ld_wrep")
            sc_rows = sc.rearrange("s j c -> (s j) c")
            for tt in range(4):
                P.dma("sp", [(scin1[0:120, :], sc_rows[tt * 120:(tt + 1) * 120, :])], writes=["RK"], key="ld_scin")
                P.add("dve", lambda e, tt=tt: e.tensor_tensor(out=prodb[tt][0:120, :], in0=scin1[0:120, :],
                                                             in1=wrep[0:120, :], op=ALU.mult),
                      ["RK"], ["RV"])

        mixer_pass(1552, [(0, 512)], True, True, False, 512, hook=sample_hook)

        wo_pre = [arena[:, scratch_mark + i * 4096:scratch_mark + (i + 1) * 4096].bitcast(BF).rearrange(
            "p (k n) -> p k n", n=512) for i in range(2)]
        for i in range(2):
            P.dma("pool", [(wo_pre[i], w_out_v[:, :, i * 512:(i + 1) * 512])],
                  writes=["wb%d" % (2 * i), "wb%d" % (2 * i + 1)], key="wo%d" % i)

        P.dma("sp", [(sout_d.rearrange("h k v -> k h v"), Sst3)], reads=["S"], key="st_S")

        PA3 = PA[:, :].rearrange("p (h t) -> p h t", t=128)
        PB3 = PB[:, :].rearrange("p (h t) -> p h t", t=128)
        P.add("dve", lambda e: e.tensor_copy(
            out=onehot3[0:16], in_=cst[0:16, C_ID:C_ID + 16].unsqueeze(2).to_broadcast([16, 16, 128])),
            ["cst"], ["KdT%d" % h for h in range(8)])
        ALIAS = [["Qt%d" % h for h in range(8)], ["Qt%d" % h for h in range(8)],
                 ["KdT%d" % h for h in range(8)], ["Kt%d" % h for h in range(8)]]
        PC3s = PC[:, :].rearrange("p (h t) -> p h t", t=128)
        VB = [(PA, PA3, ["PA0", "PA1"]), (PC, PC3s, ["PC0", "PC1"])]

        def vb_mm(s):
            (pt_, _, keys_) = VB[s % 2]

            def vbf(e):
                e.matmul(pt_[:, 0:512], onehot3[0:16, s, :], Vt[0:16, 4, 0:512], start=True, stop=True)
                return e.matmul(pt_[:, 512:1024], onehot3[0:16, s, :], Vt[0:16, 4, 512:1024], start=True, stop=True)
            P.add("pe", vbf, ["KdT0", "RV"], keys_)

        for s in range(4):
            bi = s % 4
            P.dma("sp", [(S0b[bi].rearrange("p (h v) -> p h v", v=128), sh[s].rearrange("h k v -> k h v"))],
                  writes=["S0_%d" % bi] + ALIAS[bi], key="ld_S0_%d" % bi)
        vb_mm(0)
        for s in range(16):
            bi = s % 4
            S0 = S0b[bi]
            k0 = "S0_%d" % bi
            S03 = S0.rearrange("p (h v) -> p h v", v=128)
            t3 = big[s % 2].rearrange("p (h v) -> p h v", v=128)
            tk = "big%d" % (s % 2)
            P.add("pool", lambda e, s=s, S03=S03: e.tensor_tensor(
                out=S03, in0=S03, in1=fgS[:, :, s:s + 1].to_broadcast([128, 8, 128]), op=ALU.mult),
                [k0, "fgS"], [k0])

            if s + 1 < 16:
                vb_mm(s + 1)

            def sfu(e, s=s, S03=S03, vb3=VB[s % 2][1]):
                ins = None
                for h in range(8):
                    ins = e.scalar_tensor_tensor(out=S03[:, h, :], in0=vb3[:, h, :], scalar=kgS[:, h, s:s + 1],
                                                 in1=S03[:, h, :], op0=ALU.mult, op1=ALU.add)
                return ins
            P.add("dve", sfu, VB[s % 2][2] + ["kgS", k0], [k0])
            Snb = Snbs[s % 2]
            sk = "Snb%d" % (s % 2)
            P.add("act", lambda e, S0=S0, Snb=Snb: e.activation(out=Snb, in_=S0, func=AF.Copy), [k0],
                  [sk] + (["Kt%d" % h for h in range(8)] if s < 2 else []))
            Snb3 = Snb.rearrange("p (h v) -> p h v", v=128)

            def osf(e, s=s, Snb3=Snb3):
                ins = None
                for h in range(8):
                    ins = e.matmul(PB3[:, h, s:s + 1], Snb3[:, h, :], Qb[:, h, 512 + s:513 + s],
                                   start=True, stop=True)
                return ins
            P.add("pe", osf, [sk] + ["Qb%d" % h for h in range(8)], ["PB0", "PB1"])
            P.dma("act", [(shn_d[s].rearrange("h k v -> k h v"), S03)], reads=[k0], key="st_" + k0)
            if s + 4 < 16:
                P.dma("sp", [(S03, sh[s + 4].rearrange("h k v -> k h v"))], writes=[k0], key="ld_" + k0)
        onorm_chain(16, 1024)

        P.barrier()
        prep_scale_on_act[0] = False
        reset(scratch_mark)
        hfT = A1.rearrange("p (k t) -> p k t", t=1040)
        wo = [alloc(4096, BF).rearrange("p (k n) -> p k n", n=512) for _ in range(4)]
        xin2 = [alloc(2048) for _ in range(2)]
        hti = [alloc(2048) for _ in range(2)]
        xnb_c = alloc(1024, BF)
        junk_c = alloc(1024, BF)
        alloc(352)
        wD01 = [alloc(2048, BF).rearrange("p (k n) -> p k n", n=256) for _ in range(2)]
        for cbk in range(2, 4):
            P.dma("pool", [(wo[cbk], w_out_v[:, :, cbk * 512:(cbk + 1) * 512])], writes=["wo%d" % cbk],
                  key="wo%d" % cbk)
        tilesC = [(i * 128, 128) for i in range(8)] + [(1024, 16)]
        deferred = []

        def c_mm(ti, cbk):
            t0, n = tilesC[ti]
            xt = xin2[ti % 2]
            kx = "xc%d" % (ti % 2)
            ht = hti[ti % 2]
            kh = "ht%d" % (ti % 2)
            (ps, pk) = next_acc()

            def cf(e):
                ins = None
                for k in range(16):
                    ins = e.matmul(ps[0:n, :], mixT[:, k, t0:t0 + n], wo[cbk][:, k, :],
                                   start=(k == 0), stop=(k == 15))
                return ins
            P.add("pe", cf, ["mixT", "wo%d" % cbk], [pk])
            P.add("dve", lambda e: e.tensor_tensor(
                out=ht[0:n, cbk * 512:(cbk + 1) * 512], in0=ps[0:n, :], in1=xt[0:n, cbk * 512:(cbk + 1) * 512],
                op=ALU.add), [pk, kx], [kh])

        def c_load(ti):
            t0, n = tilesC[ti]
            P.dma("sp", [(xin2[ti % 2][0:n, :], xall[1040 + t0:1040 + t0 + n, :])], writes=["xc%d" % (ti % 2)],
                  key="xc%d" % (ti % 2))

        def c_finish(ti):
            t0, n = tilesC[ti]
            ht = hti[ti % 2]
            kh = "ht%d" % (ti % 2)
            P.dma("sp", [(hscr[t0:t0 + n, :], ht[0:n, :])], reads=[kh], writes=["hscr%d" % ti], key="st_" + kh)
            if deferred:
                deferred.pop(0)()
            deferred.append(lambda: norm_transpose(ht, kh, n, hfT, t0, V_GFFN, "hfT", xnb_c, "xnb_c", junk_c,
                                                   ["junk_c"]))

        c_load(0)
        c_load(1)
        for cbk in range(4):
            c_mm(0, cbk)
            c_mm(1, cbk)
        c_finish(0)
        c_finish(1)
        def c_mm_pe(ti, cbk):
            t0, n = tilesC[ti]
            (ps, pk) = next_acc()

            def cf(e):
                ins = None
                for k in range(16):
                    ins = e.matmul(ps[0:n, :], mixT[:, k, t0:t0 + n], wo[cbk][:, k, :],
                                   start=(k == 0), stop=(k == 15))
                return ins
            P.add("pe", cf, ["mixT", "wo%d" % cbk], [pk])
            return ps, pk

        def c_add(ti, cbk, ps, pk):
            t0, n = tilesC[ti]
            xt = xin2[ti % 2]
            ht = hti[ti % 2]
            P.add("dve", lambda e: e.tensor_tensor(
                out=ht[0:n, cbk * 512:(cbk + 1) * 512], in0=ps[0:n, :], in1=xt[0:n, cbk * 512:(cbk + 1) * 512],
                op=ALU.add), [pk, "xc%d" % (ti % 2)], ["ht%d" % (ti % 2)])

        for ti in range(2, len(tilesC)):
            c_load(ti)
            accs_c = [c_mm_pe(ti, cbk) for cbk in range(4)]
            if deferred:
                deferred.pop(0)()
            for cbk in range(4):
                c_add(ti, cbk, *accs_c[cbk])
            c_finish(ti)
        while deferred:
            deferred.pop(0)()
        P.dma("pool", [(wD01[0], w_gate_v[:, :, 0:256])], writes=["wD0"], key="wD0")
        P.dma("pool", [(wD01[1], w_up_v[:, :, 0:256])], writes=["wD1"], key="wD1")

        P.barrier()
        reset(base_mark)
        sgD = [alloc(512) for _ in range(2)]
        wE0 = alloc(5632, BF).rearrange("p (k n) -> p k n", n=256)
        reset(base_mark + 8320)
        wE1 = alloc(5632, BF).rearrange("p (k n) -> p k n", n=256)
        wE = [wE0, wE1]
        reset(scratch_mark)
        hmid = alloc(NFF * 520, BF).rearrange("p (c t) -> p c t", t=1040)
        e_mark = mark()
        wD23 = [alloc(2048, BF).rearrange("p (k n) -> p k n", n=256) for _ in range(2)]
        wD = [wD01[0], wD01[1], wD23[0], wD23[1]]
        blocksD = [(0, 352), (352, 344), (696, 344)]
        wd_i = [0]

        def stream_D(view, c0):
            i = wd_i[0] % 4
            first = wd_i[0] < 2
            wd_i[0] += 1
            if not first:
                P.dma("pool", [(wD[i], view[:, :, c0:c0 + 256])], writes=["wD%d" % i], key="wD%d" % i)
            return wD[i], "wD%d" % i
        sg_i = [0]
        for fp in range(NFF // 2):
            wgt, kgt_ = stream_D(w_gate_v, fp * 256)
            wup, kup = stream_D(w_up_v, fp * 256)
            for m in range(2):
                fc = fp * 2 + m
                for (t0, n) in blocksD:
                    g_ps, gk = mm_feat(wgt, kgt_, m, hfT, t0, n, "hfT")
                    u_ps, uk = mm_feat(wup, kup, m, hfT, t0, n, "hfT")
                    sg = sgD[sg_i[0] % 2]
                    sk = "sgD%d" % (sg_i[0] % 2)
                    sg_i[0] += 1
                    P.add("act", lambda e, g_ps=g_ps, sg=sg, n=n: e.activation(out=sg[:, 0:n], in_=g_ps[:, 0:n],
                                                                              func=AF.Silu), [gk], [sk])
                    P.add("dve", lambda e, u_ps=u_ps, sg=sg, fc=fc, t0=t0, n=n: e.tensor_tensor(
                        out=hmid[:, fc, t0:t0 + n], in0=u_ps[:, 0:n], in1=sg[:, 0:n], op=ALU.mult),
                        [uk, sk], ["hmid"])

        P.dma("pool", [(wE[0], w_down_v[:, :, 0:256])], writes=["wE0"], key="wE0")

        P.barrier()
        reset(e_mark)
        hp = [alloc(256) for _ in range(2)]
        h2p = [alloc(256) for _ in range(2)]
        h2f = [alloc(2048) for _ in range(3)]
        gfin = alloc(2048)
        ssE = alloc(9 * 8).rearrange("p (t c) -> p t c", c=8)
        ssF = alloc(16)
        jnk = alloc(128, BF)
        P.dma("sp", [(gfin, gfin_d)], writes=["gfin"], key="gfin")
        it = [0]
        for cb in range(8):
            wb = wE[cb % 2]
            wk = "wE%d" % (cb % 2)
            if cb > 0:
                P.dma("pool", [(wb, w_down_v[:, :, cb * 256:(cb + 1) * 256])], writes=[wk], key=wk)
            for ti, (t0, n) in enumerate(tilesC):
                i = it[0] % 2
                it[0] += 1
                j = ti % 3
                (ps, pk) = next_acc()
                if cb == 7:
                    P.dma("pool", [(h2f[j][0:n, 0:1792], h2scr[t0:t0 + n, 0:1792])], reads=["h2scr%d" % ti],
                          writes=["h2f%d" % j], key="h2f%d" % j)

                def ef(e, ps=ps, t0=t0, n=n, wb=wb):
                    ins = None
                    for k in range(NFF):
                        ins = e.matmul(ps[0:n, 0:256], hmid[:, k, t0:t0 + n], wb[:, k, :],
                                       start=(k == 0), stop=(k == NFF - 1))
                    return ins
                P.add("pe", ef, ["hmid", wk], [pk])
                P.dma("act", [(hp[i][0:n, :], hscr[t0:t0 + n, cb * 256:(cb + 1) * 256])],
                      reads=["hscr%d" % ti], writes=["hp%d" % i], key="hp%d" % i)
                P.add("dve", lambda e, ps=ps, i=i, n=n: e.tensor_tensor(
                    out=h2p[i][0:n, :], in0=ps[0:n, 0:256], in1=hp[i][0:n, :], op=ALU.add),
                    [pk, "hp%d" % i], ["h2p%d" % i])
                P.add("act", lambda e, i=i, n=n, ti=ti, cb=cb: e.activation(
                    out=jnk[0:n, :], in_=h2p[i][0:n, :], func=AF.Square, accum_out=ssE[0:n, ti, cb:cb + 1]),
                    ["h2p%d" % i], ["jnk", "ssE%d" % ti])
                if cb < 7:
                    P.dma("sp", [(h2scr[t0:t0 + n, cb * 256:(cb + 1) * 256], h2p[i][0:n, :])],
                          reads=["h2p%d" % i], writes=["h2scr%d" % ti], key="st_h2p%d" % i)
                else:
                    P.add("dve", lambda e, n=n, ti=ti: e.tensor_reduce(
                        out=ssF[0:n, 0:1], in_=ssE[0:n, ti, :], axis=mybir.AxisListType.X, op=ALU.add),
                        ["ssE%d" % ti], ["ssF"])
                    P.add("act", lambda e, n=n: e.activation(out=ssF[0:n, 1:2], in_=ssF[0:n, 0:1], func=AF.Sqrt,
                                                            scale=1.0 / D, bias=EPS), ["ssF"], ["ssFb"])
                    P.add("dve", lambda e, n=n: e.reciprocal(out=ssF[0:n, 2:3], in_=ssF[0:n, 1:2]),
                          ["ssFb"], ["ssFc"])
                    P.add("dve", lambda e, n=n, j=j, i=i: e.scalar_tensor_tensor(
                        out=h2f[j][0:n, 1792:2048], in0=h2p[i][0:n, :], scalar=ssF[0:n, 2:3],
                        in1=gfin[0:n, 1792:2048], op0=ALU.mult, op1=ALU.mult),
                        ["h2p%d" % i, "ssFc", "gfin", "h2f%d" % j], ["h2f%d" % j])
                    P.add("dve", lambda e, n=n, j=j: e.scalar_tensor_tensor(
                        out=h2f[j][0:n, 0:1792], in0=h2f[j][0:n, 0:1792], scalar=ssF[0:n, 2:3],
                        in1=gfin[0:n, 0:1792], op0=ALU.mult, op1=ALU.mult),
                        ["h2f%d" % j, "ssFc", "gfin"], ["h2f%d" % j])
                    P.dma("sp", [(y_d[t0:t0 + n, :], h2f[j][0:n, :])], reads=["h2f%d" % j],
                          writes=["y%d" % ti], key="st_h2f%d" % j)

        P.wait_all_dma("sp")

        with nc.Block() as block:
            P.emit(block)
    return nc


_CACHE = {}


def _consts():
    cst = np.zeros((128, NCST), np.float32)
    cst[:, C_ID:C_ID + 128] = np.eye(128, dtype=np.float32)
    s = np.arange(128)[:, None]
    t = np.arange(128)[None, :]
    cst[:, C_MASK:C_MASK + 128] = ((s // 64 == t // 64) & (s <= t)).astype(np.float32)
    rm = np.ones((512,), np.float32)
    rm[::64] = 0.0
    cst[:, C_RM:C_RM + 512] = rm[None, :]
    cst[:, C_ONE:C_ONE + 128] = 1.0
    ind = np.zeros((128, 4, 16), np.float32)
    for tt in range(4):
        for sp in range(4):
            ind[sp * 30:(sp + 1) * 30, tt, tt * 4 + sp] = 1.0
    cst[:, C_IND:C_IND + 64] = ind.reshape(128, 64)
    return cst


def kernel(x_prompt, x_sample, state_hgrn, state_conv, meta_tokens, norm_mix_g, w_in, lb_logits,
           hgrn_norm_g, conv_w, conv_b, gn_g, gn_b, w_out, norm_ffn_g, w_ffn_gate, w_ffn_up,
           w_ffn_down, norm_final_g):
    f32 = np.float32
    x_prompt = np.asarray(x_prompt, f32)
    x_sample = np.asarray(x_sample, f32)
    state_hgrn = np.asarray(state_hgrn, f32)
    state_conv = np.asarray(state_conv, f32)
    meta = np.asarray(meta_tokens, f32)
    if "nc" not in _CACHE:
        _CACHE["nc"] = build_nc()
    nc = _CACHE["nc"]

    def chunks(v):
        v = np.asarray(v, f32).reshape(-1, 128)
        return v.T

    vecT = np.zeros((128, NV), f32)
    lbl = np.asarray(lb_logits, f32)
    vecT[:, V_LB0:V_LB0 + 8] = chunks(lbl[0])
    vecT[:, V_LB1:V_LB1 + 8] = chunks(lbl[1])
    vecT[:, V_HG:V_HG + 8] = chunks(np.asarray(hgrn_norm_g, f32)[0])
    vecT[:, V_CB:V_CB + 8] = chunks(np.asarray(conv_b, f32)[0])
    vecT[:, V_GNG:V_GNG + 8] = chunks(np.asarray(gn_g, f32)[0])
    vecT[:, V_GNB:V_GNB + 8] = chunks(np.asarray(gn_b, f32)[0])
    vecT[:, V_GMIX:V_GMIX + 16] = chunks(np.asarray(norm_mix_g, f32)[0])
    vecT[:, V_GFFN:V_GFFN + 16] = chunks(np.asarray(norm_ffn_g, f32)[0])
    cw = np.asarray(conv_w, f32)[0]
    vecT[:, V_CW:V_CW + 248] = cw.reshape(31, 8, 128).transpose(2, 1, 0).reshape(128, 248)
    vecT[:, V_W30:V_W30 + 8] = chunks(cw[30])
    cst = _consts()
    gfin = np.ascontiguousarray(np.broadcast_to(np.asarray(norm_final_g, f32)[None, :], (128, D)))
    wrep = np.ascontiguousarray(np.tile(cw[0:30], (4, 1)))
    w_in0 = np.ascontiguousarray(np.asarray(w_in, f32)[0])
    w_out0 = np.ascontiguousarray(np.asarray(w_out, f32)[0])
    w_g0 = np.ascontiguousarray(np.asarray(w_ffn_gate, f32)[0])
    w_u0 = np.ascontiguousarray(np.asarray(w_ffn_up, f32)[0])
    w_d0 = np.ascontiguousarray(np.asarray(w_ffn_down, f32)[0])

    in_maps = []
    for c in range(8):
        b, j = c // 2, c % 2
        hp = np.concatenate([meta, x_prompt[b]], axis=0)
        if j == 0:
            xpre = np.concatenate([np.zeros((1024, D), f32), meta], axis=0)
            xmain = hp[16:1040]
        else:
            xpre = hp[0:1040]
            xmain = hp[1040:2064]
        xs = x_sample[16 * c:16 * c + 16, 0, :]
        xall = np.ascontiguousarray(np.concatenate([xpre, xmain, xs], axis=0))
        in_maps.append({
            "xall": xall,
            "sh": np.ascontiguousarray(state_hgrn[0, 16 * c:16 * c + 16]),
            "sc": np.ascontiguousarray(state_conv[0, 16 * c:16 * c + 16]),
            "vecT": vecT, "cst": cst, "gfin": gfin, "wrep": wrep,
            "w_in": w_in0, "w_out": w_out0, "w_gate": w_g0, "w_up": w_u0, "w_down": w_d0,
        })
    res = run_bass_kernel_spmd(nc, in_maps, core_ids=list(range(8)))
    R = res.results
    y_prompt = np.zeros((4, 2048, D), f32)
    y_sample = np.zeros((128, 1, D), f32)
    nshp = np.zeros((1, 4, 8, 128, 128), f32)
    nscp = np.zeros((1, 4, 30, 1024), f32)
    nshs = np.zeros((1, 128, 8, 128, 128), f32)
    nscs = np.zeros((1, 128, 30, 1024), f32)
    for c in range(8):
        b, j = c // 2, c % 2
        r = R[c]
        y_prompt[b, j * 1024:(j + 1) * 1024] = r["y"][0:1024]
        y_sample[16 * c:16 * c + 16, 0] = r["y"][1024:1040]
        nshs[0, 16 * c:16 * c + 16] = r["shn"]
        nscs[0, 16 * c:16 * c + 16] = r["scn"]
        if j == 1:
            nshp[0, b] = r["sout"]
            nscp[0, b] = r["ctail"]
    return (y_prompt, y_sample, nshp, nscp, nshs, nscs)
```
